# Optimizing a Trainium2 kernel written in Bass

```python
import jax, jax.numpy as jnp
from jax import lax
import numpy as np

D_MODEL = 1024
BATCH = 8
SEQ = 2048
DEPTH = 1

MIX_WIDTH = D_MODEL
HEAD_DIM = 64
RWKV_WIDTH = MIX_WIDTH // 2
ATTN_WIDTH = MIX_WIDTH - RWKV_WIDTH
RWKV_HEADS = RWKV_WIDTH // HEAD_DIM
ATTN_HEADS = ATTN_WIDTH // HEAD_DIM
DECAY_LORA = 64
AAA_LORA = 64
GATE_LORA = 128
DILATED_PAIRS = ((128, 1), (512, 4), (2048, 16))
ATTN_BLOCK = 128
D_FF = -(-8 * D_MODEL // (3 * 256)) * 256
NORM_EPS = 1e-6
GN_EPS = 64e-5
RWKV_SHIFT_COLS = 3 * RWKV_WIDTH + DECAY_LORA + AAA_LORA + GATE_LORA
IN_COLS = RWKV_SHIFT_COLS + 3 * ATTN_WIDTH

kernel_name = 'hybrid_rwkv7_dilated_attn'


def rmsnorm(x, g):
    xf = x.astype(jnp.float32)
    y = xf * lax.rsqrt(jnp.mean(xf * xf, axis=-1, keepdims=True) + NORM_EPS)
    return (y * g.astype(jnp.float32)).astype(x.dtype)


def wkv7_scan(r, w, k, v, kk, a):
    B, S, H, N = r.shape

    def step(state, inp):
        r_t, w_t, k_t, v_t, kk_t, a_t = inp
        sa = jnp.einsum('bhij,bhj->bhi', state, -kk_t)
        state = (state * w_t[:, :, None, :]
                 + sa[..., :, None] * (kk_t * a_t)[..., None, :]
                 + v_t[..., :, None] * k_t[..., None, :])
        y = jnp.einsum('bhij,bhj->bhi', state, r_t)
        return state, y

    xs = tuple(t.transpose(1, 0, 2, 3) for t in (r, w, k, v, kk, a))
    init = jnp.zeros((B, H, N, N), jnp.float32)
    _, ys = lax.scan(step, init, xs)
    return ys.transpose(1, 0, 2, 3)


def rwkv7_mixer(p, mu, w0, w2, a0, a2, g2, k_k, k_a, r_k, ln_w, ln_b):
    B, S, _ = p.shape
    prev = jnp.pad(p[:, :-1], ((0, 0), (1, 0), (0, 0)))
    p = p + (prev - p) * mu
    W = RWKV_WIDTH
    r, k, v, xw, xa, xg = jnp.split(
        p, [W, 2 * W, 3 * W, 3 * W + DECAY_LORA, 3 * W + DECAY_LORA + AAA_LORA], axis=-1)
    w = -jax.nn.softplus(-(w0 + jnp.tanh(xw) @ w2)) - 0.5
    decay = jnp.exp(-jnp.exp(w.astype(jnp.float32)))
    a = jax.nn.sigmoid(a0 + xa @ a2)
    g = jax.nn.sigmoid(xg) @ g2

    def hd(t):
        return t.reshape(B, S, RWKV_HEADS, HEAD_DIM).astype(jnp.float32)

    kk = hd(k * k_k)
    kk = kk / jnp.maximum(jnp.sqrt(jnp.sum(kk * kk, axis=-1, keepdims=True)), 1e-12)
    k = k * (1 + (a - 1) * k_a)
    rh, kh, vh, ah, wh = hd(r), hd(k), hd(v), hd(a), hd(decay)
    y = wkv7_scan(rh, wh, kh, vh, kk, ah)
    mean = jnp.mean(y, axis=-1, keepdims=True)
    var = jnp.mean(jnp.square(y - mean), axis=-1, keepdims=True)
    y = ((y - mean) * lax.rsqrt(var + GN_EPS)).reshape(B, S, W)
    y = y * ln_w.astype(jnp.float32) + ln_b.astype(jnp.float32)
    bonus = jnp.sum(rh * kh * r_k.astype(jnp.float32), axis=-1, keepdims=True) * vh
    y = y + bonus.reshape(B, S, W)
    return (y * g.astype(jnp.float32)).astype(p.dtype)


def dilated_branch(q, k, v, window, dilation):
    B, H, S, Dh = q.shape
    L = S // dilation
    span = window // dilation
    nb = -(-L // ATTN_BLOCK)
    Lp = nb * ATTN_BLOCK

    def to_sub(t):
        t = t.reshape(B, H, L, dilation, Dh).transpose(0, 1, 3, 2, 4)
        t = jnp.pad(t, ((0, 0), (0, 0), (0, 0), (0, Lp - L), (0, 0)))
        return t.reshape(B, H, dilation, nb, ATTN_BLOCK, Dh)

    def with_prev(t):
        prv = jnp.pad(t[:, :, :, :-1], ((0, 0), (0, 0), (0, 0), (1, 0), (0, 0), (0, 0)))
        return jnp.concatenate([prv, t], axis=4)

    qb = to_sub(q)
    kc = with_prev(to_sub(k))
    vc = with_prev(to_sub(v))
    s = jnp.einsum('bhrnqe,bhrnke->bhrnqk', qb, kc) * (Dh ** -0.5)
    qi = jnp.arange(ATTN_BLOCK)[:, None] + ATTN_BLOCK
    kj = jnp.arange(2 * ATTN_BLOCK)[None, :]
    rel = qi - kj
    blk = jnp.arange(nb)[:, None, None]
    valid = (rel >= 0) & (rel <= span) & ((blk - 1) * ATTN_BLOCK + kj >= 0)
    s = jnp.where(valid, s, -jnp.inf)
    m = jnp.max(s, axis=-1, keepdims=True)
    pe = jnp.exp(s - m)
    den = jnp.sum(pe, axis=-1, keepdims=True)
    o = jnp.einsum('bhrnqk,bhrnke->bhrnqe', pe, vc) / den
    lse = (m + jnp.log(den))[..., 0]
    o = o.reshape(B, H, dilation, Lp, Dh)[:, :, :, :L].transpose(0, 1, 3, 2, 4).reshape(B, H, S, Dh)
    lse = lse.reshape(B, H, dilation, Lp)[..., :L].transpose(0, 1, 3, 2).reshape(B, H, S)
    return o, lse


def dilated_attention(q, k, v, out_g):
    B, S, _ = q.shape

    def heads(t):
        return t.reshape(B, S, ATTN_HEADS, HEAD_DIM).transpose(0, 2, 1, 3).astype(jnp.float32)

    qh, kh, vh = heads(q), heads(k), heads(v)
    outs, lses = [], []
    for window, dilation in DILATED_PAIRS:
        o, l = dilated_branch(qh, kh, vh, window, dilation)
        outs.append(o)
        lses.append(l)
    alpha = jax.nn.softmax(jnp.stack(lses), axis=0)
    o = jnp.sum(alpha[..., None] * jnp.stack(outs), axis=0).transpose(0, 2, 1, 3)
    o = o * lax.rsqrt(jnp.mean(o * o, axis=-1, keepdims=True) + NORM_EPS)
    o = o.reshape(B, S, ATTN_WIDTH) * out_g.astype(jnp.float32)
    return o.astype(q.dtype)


def setup_inputs(seed: int = 0) -> dict:
    key = jax.random.key(seed)
    ks = jax.random.split(key, 24)
    nrm = jax.random.normal
    f32 = jnp.float32
    return {
        'x': nrm(ks[0], (BATCH, SEQ, D_MODEL), f32),
        'mix_norm_g': 1.0 + 0.02 * nrm(ks[1], (DEPTH, D_MODEL), f32),
        'w_in': nrm(ks[2], (DEPTH, D_MODEL, IN_COLS), f32) * D_MODEL ** -0.5,
        'mu_shift': jax.random.uniform(ks[3], (DEPTH, RWKV_SHIFT_COLS), f32),
        'decay_w0': jax.random.uniform(ks[4], (DEPTH, RWKV_WIDTH), f32, minval=-6.0, maxval=-1.0),
        'decay_w2': nrm(ks[5], (DEPTH, DECAY_LORA, RWKV_WIDTH), f32) * 0.1 * DECAY_LORA ** -0.5,
        'iclr_a0': 0.1 * nrm(ks[6], (DEPTH, RWKV_WIDTH), f32),
        'iclr_a2': nrm(ks[7], (DEPTH, AAA_LORA, RWKV_WIDTH), f32) * 0.1 * AAA_LORA ** -0.5,
        'gate_g2': nrm(ks[8], (DEPTH, GATE_LORA, RWKV_WIDTH), f32) * GATE_LORA ** -0.5,
        'k_k': 0.85 + 0.02 * nrm(ks[9], (DEPTH, RWKV_WIDTH), f32),
        'k_a': 1.0 + 0.02 * nrm(ks[10], (DEPTH, RWKV_WIDTH), f32),
        'r_k': 0.1 * nrm(ks[11], (DEPTH, RWKV_HEADS, HEAD_DIM), f32),
        'ln_x_w': 1.0 + 0.02 * nrm(ks[12], (DEPTH, RWKV_WIDTH), f32),
        'ln_x_b': 0.02 * nrm(ks[13], (DEPTH, RWKV_WIDTH), f32),
        'attn_out_g': 1.0 + 0.02 * nrm(ks[14], (DEPTH, ATTN_WIDTH), f32),
        'w_out': nrm(ks[15], (DEPTH, MIX_WIDTH, D_MODEL), f32) * MIX_WIDTH ** -0.5,
        'ffn_norm_g': 1.0 + 0.02 * nrm(ks[16], (DEPTH, D_MODEL), f32),
        'w_gate': nrm(ks[17], (DEPTH, D_MODEL, D_FF), f32) * D_MODEL ** -0.5,
        'w_up': nrm(ks[18], (DEPTH, D_MODEL, D_FF), f32) * D_MODEL ** -0.5,
        'w_down': nrm(ks[19], (DEPTH, D_FF, D_MODEL), f32) * D_FF ** -0.5,
        'final_norm_g': 1.0 + 0.02 * nrm(ks[20], (D_MODEL,), f32),
    }


def reference(x, mix_norm_g, w_in, mu_shift, decay_w0, decay_w2, iclr_a0, iclr_a2, gate_g2,
              k_k, k_a, r_k, ln_x_w, ln_x_b, attn_out_g, w_out, ffn_norm_g, w_gate, w_up,
              w_down, final_norm_g):
    c0 = RWKV_SHIFT_COLS
    for i in range(DEPTH):
        h = rmsnorm(x, mix_norm_g[i])
        proj = h @ w_in[i]
        p_a, q, k, v = jnp.split(proj, [c0, c0 + ATTN_WIDTH, c0 + 2 * ATTN_WIDTH], axis=-1)
        y_a = rwkv7_mixer(p_a, mu_shift[i], decay_w0[i], decay_w2[i], iclr_a0[i], iclr_a2[i],
                          gate_g2[i], k_k[i], k_a[i], r_k[i], ln_x_w[i], ln_x_b[i])
        y_b = dilated_attention(q, k, v, attn_out_g[i])
        x = x + jnp.concatenate([y_a, y_b.astype(y_a.dtype)], axis=-1) @ w_out[i]
        h = rmsnorm(x, ffn_norm_g[i])
        x = x + (jax.nn.silu(h @ w_gate[i]) * (h @ w_up[i])) @ w_down[i]
    return rmsnorm(x, final_norm_g)
```

```python
import contextlib
import numpy as np
import concourse.bass as bass
import concourse.mybir as mybir
from concourse.bass_utils import run_bass_kernel_spmd

F32 = mybir.dt.float32
BF16 = mybir.dt.bfloat16
AF = mybir.ActivationFunctionType
ALU = mybir.AluOpType

PE, ACT, DVE, POOL, SP = "tensor", "scalar", "vector", "gpsimd", "sync"
ENGINES = (PE, ACT, DVE, POOL, SP)

S = 2048
D = 1024
DFF = 2816
NFF = 22
INC = 3328
CEXP = float(np.exp(-0.5))
NEG = -30000.0
MU0, W00, A00, KK0, KA0, RK0, LNW0, LNB0, OG0, NPC = 0, 14, 18, 22, 26, 30, 34, 38, 42, 46

DEBUG = {}


class Tok:
    __slots__ = ("w", "r", "name")

    def __init__(self, name=""):
        self.w = None
        self.r = {}
        self.name = name


class Prog:
    def __init__(self, nc):
        self.nc = nc
        self.streams = {e: [] for e in ENGINES}
        self.seq = {e: 0 for e in ENGINES}
        self.dma_cnt = {}
        self.out_dma = []
        self.needed = {e: set() for e in ENGINES}
        self.pending = {e: None for e in ENGINES}
        self.nuniq = 0

    def _gather(self, eng, reads, writes):
        deps = set()
        for t in reads:
            if t.w is not None:
                deps.add(t.w)
        for t in writes:
            if t.w is not None:
                deps.add(t.w)
            deps.update(t.r.values())
        if self.pending[eng] is not None:
            deps.update(self.pending[eng])
            self.pending[eng] = None
        out = []
        for d in deps:
            if d[0] == 'e' and d[1] == PE and eng == PE:
                continue
            out.append(d)
            if d[0] == 'e':
                self.needed[d[1]].add(d[2])
        return out

    def _mark(self, me, reads, writes):
        key = (me[0], me[1])
        for t in reads:
            t.r[key] = me
        for t in writes:
            t.w = me
            t.r = {}

    def op(self, eng, fn, reads=(), writes=()):
        deps = self._gather(eng, reads, writes)
        self.seq[eng] += 1
        me = ('e', eng, self.seq[eng])
        self._mark(me, reads, writes)
        self.streams[eng].append((deps, fn, me))
        return me

    def dma(self, eng, fn, semkey, reads=(), writes=(), is_out=False):
        deps = self._gather(eng, reads, writes)
        self.dma_cnt[semkey] = self.dma_cnt.get(semkey, 0) + 1
        me = ('d', semkey, self.dma_cnt[semkey])
        self._mark(me, reads, writes)
        self.streams[eng].append((deps, fn, me))
        if is_out:
            self.out_dma.append(me)
        return me

    def barrier(self):
        deps = set()
        for e in ENGINES:
            if e != SP and self.seq[e] > 0:
                deps.add(('e', e, self.seq[e]))
        for k, c in self.dma_cnt.items():
            deps.add(('d', k, c))
        for e in ENGINES:
            cur = self.pending[e] or set()
            self.pending[e] = set(cur) | deps

    def replay(self):
        nc = self.nc
        ranks = {}
        for e in ENGINES:
            srt = sorted(self.needed[e])
            ranks[e] = {s: i + 1 for i, s in enumerate(srt)}
        with contextlib.ExitStack() as st:
            esem = {e: st.enter_context(nc.semaphore("s_" + e)) for e in ENGINES}
            dsem = {k: st.enter_context(nc.semaphore("d_%s" % (k,))) for k in self.dma_cnt}
            block = st.enter_context(nc.Block())

            def resolve(d):
                if d[0] == 'e':
                    return esem[d[1]], ranks[d[1]][d[2]]
                return dsem[d[1]], 16 * d[2]

            def run(ename, handle):
                waited = {}
                for deps, fn, me in self.streams[ename]:
                    for d in deps:
                        sem, val = resolve(d)
                        k = (d[0], d[1])
                        if waited.get(k, 0) < val:
                            handle.wait_ge(sem, val)
                            waited[k] = val
                    inst = fn(handle)
                    if me[0] == 'e':
                        if me[2] in ranks[ename]:
                            inst.then_inc(esem[ename], 1)
                    else:
                        inst.then_inc(dsem[me[1]], 16)
                if ename == SP:
                    for d in self.out_dma:
                        sem, val = resolve(d)
                        handle.wait_ge(sem, val)

            @block.tensor
            def _(h):
                run(PE, h)

            @block.scalar
            def _(h):
                run(ACT, h)

            @block.vector
            def _(h):
                run(DVE, h)

            @block.gpsimd
            def _(h):
                run(POOL, h)

            @block.sync
            def _(h):
                run(SP, h)

    def mm(self, out, lhsT, rhs, R, W, start=True, stop=True, tp=None, sgc=False):
        if sgc:
            return self.op(PE, lambda e: e.matmul(out, lhsT=lhsT, rhs=rhs, start=start, stop=stop, skip_group_check=True), R, W)
        if tp is None:
            return self.op(PE, lambda e: e.matmul(out, lhsT=lhsT, rhs=rhs, start=start, stop=stop), R, W)
        return self.op(PE, lambda e: e.matmul(out, lhsT=lhsT, rhs=rhs, start=start, stop=stop, tile_position=tp), R, W)

    def tr(self, out, in_, ident, R, W):
        return self.op(PE, lambda e: e.transpose(out=out, in_=in_, identity=ident), R, W)

    def act(self, out, in_, func, R, W, bias=None, scale=None, accum=None):
        kw = {}
        if bias is not None:
            kw["bias"] = bias
        if scale is not None:
            kw["scale"] = scale
        if accum is not None:
            kw["accum_out"] = accum
        return self.op(ACT, lambda e: e.activation(out=out, in_=in_, func=func, **kw), R, W)

    def tt(self, eng, out, a, b, op, R, W):
        return self.op(eng, lambda e: e.tensor_tensor(out=out, in0=a, in1=b, op=op), R, W)

    def ts(self, eng, out, a, s1, op0, R, W, s2=None, op1=None):
        if op1 is None:
            return self.op(eng, lambda e: e.tensor_scalar(out=out, in0=a, scalar1=s1, scalar2=None, op0=op0), R, W)
        return self.op(eng, lambda e: e.tensor_scalar(out=out, in0=a, scalar1=s1, scalar2=s2, op0=op0, op1=op1), R, W)

    def stt(self, out, a, s, b, op0, op1, R, W):
        return self.op(DVE, lambda e: e.scalar_tensor_tensor(out=out, in0=a, scalar=s, in1=b, op0=op0, op1=op1), R, W)

    def copy(self, eng, out, in_, R, W):
        if eng == ACT:
            return self.op(ACT, lambda e: e.copy(out=out, in_=in_), R, W)
        return self.op(eng, lambda e: e.tensor_copy(out=out, in_=in_), R, W)

    def memset(self, eng, ap, val, W):
        return self.op(eng, lambda e: e.memset(ap, val), (), W)

    def recip(self, out, in_, R, W):
        return self.op(DVE, lambda e: e.reciprocal(out=out, in_=in_), R, W)


class Arena:
    def __init__(self, nc, base=16640, top=229000):
        self.nc, self.base, self.top, self.cur, self.n = nc, base, top, base, 0

    def alloc(self, shape, dtype, name="t"):
        nbytes = int(np.prod(shape[1:])) * (4 if dtype == F32 else 2)
        nbytes = (nbytes + 63) // 64 * 64
        if self.cur + nbytes > self.top:
            raise RuntimeError("arena overflow %s %d" % (name, self.cur + nbytes - self.top))
        self.n += 1
        t = self.nc.alloc_sbuf_tensor_at("%s_%d" % (name, self.n), list(shape), dtype, offset=self.cur)
        self.cur += nbytes
        return t.ap()

    def mark(self):
        return self.cur

    def reset(self, m):
        self.cur = m


def build(dbg=None, upto='Z'):
    nc = bass.Bass("TRN2", target_bir_lowering=False)
    P = Prog(nc)
    A = Arena(nc)

    def dram(name, shape, kind="ExternalInput"):
        return nc.dram_tensor(name, list(shape), F32, kind=kind).ap()

    x_d = dram("x", [S, D])
    w_in_d = dram("w_in", [D, INC])
    w_out_d = dram("w_out", [D, D])
    w_gate_d = dram("w_gate", [D, DFF])
    w_up_d = dram("w_up", [D, DFF])
    w_down_d = dram("w_down", [DFF, D])
    pcols_d = dram("pcols", [128, NPC])
    gbc_d = dram("gbc", [3, 128, D])
    lora_d = dram("lora", [128, 1024])
    out_d = dram("out", [S, D], kind="ExternalOutput")
    dbg_d = {}
    if dbg:
        for k, shp in dbg.items():
            dt_ = BF16 if k in ("hT", "yT0", "yT4") else F32
            dbg_d[k] = nc.dram_tensor("dbg_" + k, list(shp), dt_, kind="ExternalOutput").ap()

    x_v = x_d.rearrange("(t p) d -> p t d", p=128)
    out_v = out_d.rearrange("(t p) d -> p t d", p=128)
    w_in_v = w_in_d.rearrange("(c p) n -> p c n", p=128)
    w_out_v = w_out_d.rearrange("(c p) n -> p c n", p=128)
    w_gate_v = w_gate_d.rearrange("(c p) n -> p c n", p=128)
    w_up_v = w_up_d.rearrange("(c p) n -> p c n", p=128)
    w_down_v = w_down_d.rearrange("(c p) n -> p c n", p=128)

    banks = [(nc.alloc_psum_tensor("ps%d" % i, [128, 512], F32).ap(), Tok("ps%d" % i)) for i in range(8)]
    bank_i = [0]

    def psum():
        b = banks[bank_i[0] % 8]
        bank_i[0] += 1
        return b

    dbg_dumps = []

    def dump(key, ap, tok):
        if dbg and key in dbg_d:
            P.dma(SP, lambda e: e.dma_start(out=dbg_d[key], in_=ap), "dbg_" + key, reads=[tok], is_out=True)

    tC = Tok("consts")
    pcols = A.alloc([128, NPC], F32, "pcols")
    omm = A.alloc([128, 14], F32, "omm")
    omka = A.alloc([128, 4], F32, "omka")
    g1bc = A.alloc([128, D], F32, "g1bc")
    g2bc = A.alloc([128, D], F32, "g2bc")
    gfbc = A.alloc([128, D], F32, "gfbc")
    lora = A.alloc([128, 1024], F32, "lora")
    ones = A.alloc([128, 128], F32, "ones")
    zeros = A.alloc([128, 128], F32, "zeros")
    ident = A.alloc([128, 128], F32, "ident")
    identb = A.alloc([128, 128], BF16, "identb")
    identpair = A.alloc([128, 64], F32, "identpair")
    BD1 = A.alloc([128, 128], F32, "bd1")
    BD64 = A.alloc([128, 128], F32, "bd64")
    maskA = A.alloc([128, 2, 2, 128], F32, "maskA")
    maskL = A.alloc([128, 2, 128], F32, "maskL")
    rmask = A.alloc([128, 256], F32, "rmask")
    MB_TT = A.alloc([128, 512], BF16, "MB_TT")
    MB_FT = A.alloc([128, 512], BF16, "MB_FT")
    MB_FF = A.alloc([128, 512], BF16, "MB_FF")
    mtmp = A.alloc([128, 256], F32, "mtmp")
    onespad = A.alloc([128, 2, 128], BF16, "onespad")
    nwa = A.alloc([128, 8], F32, "nwa")
    onec = A.alloc([128, 1], F32, "onec")
    tinyc = A.alloc([128, 1], F32, "tinyc")
    epsN = A.alloc([128, 1], F32, "epsN")
    epsG = A.alloc([128, 1], F32, "epsG")
    ssq = A.alloc([128, 64], F32, "ssq")
    tS = Tok("ssq")

    P.dma(SP, lambda e: e.dma_start(out=pcols, in_=pcols_d), "consts", writes=[tC])
    P.dma(SP, lambda e: e.dma_start(out=g1bc, in_=gbc_d[0]), "consts", writes=[tC])
    P.dma(SP, lambda e: e.dma_start(out=g2bc, in_=gbc_d[1]), "consts", writes=[tC])
    P.dma(SP, lambda e: e.dma_start(out=gfbc, in_=gbc_d[2]), "consts", writes=[tC])
    P.dma(SP, lambda e: e.dma_start(out=lora, in_=lora_d), "consts", writes=[tC])
    P.memset(POOL, ones, 1.0, [tC])
    P.memset(POOL, zeros, 0.0, [tC])
    P.memset(POOL, epsN, 1e-6, [tC])
    P.memset(POOL, epsG, 64e-5, [tC])
    P.memset(POOL, BD1, 0.0, [tC])
    P.memset(POOL, BD64, 0.0, [tC])
    P.memset(POOL, onespad, 0.0, [tC])
    for q in range(2):
        P.memset(POOL, BD1[64 * q:64 * q + 64, 64 * q:64 * q + 64], 1.0, [tC])
        P.memset(POOL, BD64[64 * q:64 * q + 64, 64 * q:64 * q + 64], 1.0 / 64, [tC])
        P.memset(POOL, onespad[:, q, 64 * q:64 * q + 64], 1.0, [tC])
    P.memset(POOL, rmask, 1.0, [tC])
    P.memset(POOL, rmask[:, 0:1], 0.0, [tC])
    P.memset(POOL, rmask[:, 128:129], 0.0, [tC])

    def asel(out, in_, cmp, fill, base, cm, step):
        P.op(POOL, lambda e: e.affine_select(out=out, in_=in_, pattern=[[step, 128]], compare_op=cmp, fill=fill,
                                             base=base, channel_multiplier=cm), [tC], [tC])

    asel(ident, ones, ALU.is_equal, 0.0, 0, -1, 1)
    for u in range(2):
        asel(maskA[:, u, 0, :], ones, ALU.is_ge, 0.0, -1, -1, 1)
        asel(maskA[:, u, 1, :], ones, ALU.is_ge, 0.0, 0, -1, 1)
        asel(maskL[:, u, :], ones, ALU.is_ge, 0.0, -1, 1, -1)
    asel(mtmp[:, 0:128], ones, ALU.is_ge, 0.0, 0, 1, -1)
    asel(mtmp[:, 128:256], ones, ALU.is_ge, 0.0, 0, -1, 1)
    for h2 in range(2):
        P.copy(POOL, MB_TT[:, 256 * h2:256 * h2 + 256], mtmp, [tC], [tC])
        P.memset(POOL, MB_FF[:, 256 * h2:256 * h2 + 128], 0.0, [tC])
        P.copy(POOL, MB_FF[:, 256 * h2 + 128:256 * h2 + 256], mtmp[:, 128:256], [tC], [tC])
    P.memset(POOL, MB_FT[:, 0:128], 0.0, [tC])
    P.copy(POOL, MB_FT[:, 128:256], mtmp[:, 128:256], [tC], [tC])
    P.copy(POOL, MB_FT[:, 256:512], mtmp, [tC], [tC])
    P.copy(POOL, identb, ident, [tC], [tC])
    P.tt(POOL, identpair, ident[:, 0:64], ident[:, 64:128], ALU.add, [tC], [tC])
    P.ts(DVE, omm, pcols[:, MU0:MU0 + 14], -1.0, ALU.mult, [tC], [tC], s2=1.0, op1=ALU.add)
    P.ts(DVE, omka, pcols[:, KA0:KA0 + 4], -1.0, ALU.mult, [tC], [tC], s2=1.0, op1=ALU.add)
    P.ts(DVE, nwa, pcols[:, W00:W00 + 8], -1.0, ALU.mult, [tC], [tC])
    P.memset(DVE, onec, 1.0, [tC])
    P.memset(DVE, tinyc, 1e-24, [tC])

    persist_mark = A.mark()

    hT = A.alloc([128, 8, S], BF16, "hT")
    yT = A.alloc([128, 8, S], BF16, "yT")
    t_hT = [Tok("hT%d" % i) for i in range(16)]
    t_yT = [Tok("yT%d" % i) for i in range(8)]
    phase_mark = A.mark()

    tSs = [Tok("ssq%d" % i) for i in range(64)]

    def norm_p1(src_ap, src_tok, gbc, scr, ncol):
        hn, t_hn, sq, t_sq = scr
        tS_ = tSs[ncol]
        P.act(sq, src_ap, AF.Square, [src_tok], [t_sq, tS_], accum=ssq[:, ncol:ncol + 1])
        P.act(ssq[:, ncol:ncol + 1], ssq[:, ncol:ncol + 1], AF.Ln, [tS_, tC], [tS_], bias=epsN, scale=1.0 / D)
        P.act(ssq[:, ncol:ncol + 1], ssq[:, ncol:ncol + 1], AF.Exp, [tS_], [tS_], scale=-0.5)
        P.stt(hn, src_ap, ssq[:, ncol:ncol + 1], gbc, ALU.mult, ALU.mult, [src_tok, tS_, tC], [t_hn])

    def norm_p2(scr, dstT, dst_col0, dst_tok, psfn):
        hn, t_hn, sq, t_sq = scr
        for half in range(2):
            ps, tp = psfn()
            for j in range(4):
                dc = half * 4 + j
                P.tr(ps[:, 128 * j:128 * j + 128], hn[:, 128 * dc:128 * dc + 128], ident, [t_hn, tC], [tp])
            P.copy(ACT if half == 0 else DVE, dstT[:, half * 4:half * 4 + 4, dst_col0:dst_col0 + 128],
                   ps.rearrange("p (j t) -> p j t", j=4), [tp], [dst_tok])

    NXT = 4
    NSC = 3
    A.cur = (A.top - (NXT * 4096 + NSC * (4096 + 2048)) - 512) // 64 * 64
    xt = [(A.alloc([128, D], F32, "xt"), Tok("xt")) for _ in range(NXT)]
    scrA = [(A.alloc([128, D], F32, "hn"), Tok("hn"), A.alloc([128, D], BF16, "sq"), Tok("sq")) for _ in range(NSC)]
    def loadx(tt_):
        xa, txa = xt[tt_ % NXT]
        P.dma(SP, (lambda xa, i: lambda e: e.dma_start(out=xa, in_=x_v[:, i, :]))(xa, tt_), "xt%d" % (tt_ % NXT),
              writes=[txa])
        return xa, txa

    xq = {}
    for tt_ in range(min(NXT, 16)):
        xq[tt_] = loadx(tt_)
    for tt_ in range(2):
        norm_p1(xq[tt_][0], xq[tt_][1], g1bc, scrA[tt_ % NSC], tt_)
    for tt_ in range(16):
        if tt_ + 2 < 16:
            if tt_ + 2 not in xq:
                xq[tt_ + 2] = loadx(tt_ + 2)
            xa, txa = xq[tt_ + 2]
            norm_p1(xa, txa, g1bc, scrA[(tt_ + 2) % NSC], tt_ + 2)
        norm_p2(scrA[tt_ % NSC], hT, tt_ * 128, t_hT[tt_], psum)
    dump("hT", hT[:, 0, :], t_hT[15])
    phaseA_lo = (A.top - (NXT * 4096 + NSC * (4096 + 2048)) - 512) // 64 * 64
    A.reset(phase_mark)
    if upto == 'A':
        P.barrier()
        P.replay()
        return nc

    hT_all = t_hT
    W = 256
    NB = S // W
    sh12 = A.alloc([128, S], F32, "sh12")
    sh13 = A.alloc([128, S], F32, "sh13")
    t_sh = [Tok("sh%d" % i) for i in range(4)]
    mB0 = A.mark()
    wsh = [(A.alloc([128, 8, 128], BF16, "wsh"), Tok("wsh")) for _ in range(2)]
    prsb = [(A.alloc([128, 513], F32, "prs"), Tok("prs")) for _ in range(2)]
    t1sb = [(A.alloc([128, 512], F32, "t1s"), Tok("t1s")) for _ in range(2)]
    for ci, cc in enumerate((12, 13)):
        wb, twb = wsh[ci]
        P.dma(POOL, (lambda wb, cc: lambda e: e.dma_start(out=wb, in_=w_in_v[:, :, cc * 128:cc * 128 + 128]))(wb, cc),
              "wsh%d" % ci, writes=[twb])
    for ci, cc in enumerate((12, 13)):
        wb, twb = wsh[ci]
        P.memset(DVE, prsb[0][0][:, 0:1], 0.0, [prsb[0][1]])
        for b4 in range(4):
            prs, t_prs = prsb[b4 % 2]
            prn, t_prn = prsb[(b4 + 1) % 2]
            t1s, t_t1s = t1sb[b4 % 2]
            ps, tp = psum()
            for dc in range(8):
                P.mm(ps, wb[:, dc, :], hT[:, dc, b4 * 512:b4 * 512 + 512], [twb] + hT_all[b4 * 4:b4 * 4 + 4], [tp],
                     start=(dc == 0), stop=(dc == 7))
            P.copy(ACT, prs[:, 1:513], ps, [tp], [t_prs])
            P.copy(ACT, prn[:, 0:1], prs[:, 512:513], [t_prs], [t_prn])
            P.ts(DVE, t1s, prs[:, 0:512], pcols[:, MU0 + cc:MU0 + cc + 1], ALU.mult, [t_prs, tC], [t_t1s])
            dst = sh12 if cc == 12 else sh13
            sl = slice(b4 * 512, b4 * 512 + 512)
            P.stt(t1s, prs[:, 1:513], omm[:, cc:cc + 1], t1s, ALU.mult, ALU.add, [t_prs, t_t1s, tC], [t_t1s])
            if cc == 12:
                P.act(dst[0:64, sl], t1s[0:64, :], AF.Tanh, [t_t1s], [t_sh[b4]])
                P.copy(DVE, dst[64:128, sl], t1s[64:128, :], [t_t1s], [t_sh[b4]])
            else:
                P.act(dst[:, sl], t1s, AF.Sigmoid, [t_t1s], [t_sh[b4]])
    assert A.mark() <= phaseA_lo, "prepass buffers overlap the still-live phase A buffers"
    P.barrier()
    A.reset(mB0)

    if upto == 'B0':
        P.replay()
        return nc
    wrkv = [[(A.alloc([128, 8, 128], BF16, "wrkv"), Tok("wrkv")) for _ in range(3)] for _ in range(2)]
    MT_all = A.alloc([128, 16, 64], F32, "MT")
    Nn_all = A.alloc([128, 16, 64], F32, "Nn")
    RhT_all = A.alloc([128, 16, 128], F32, "RhT")
    YlT_all = A.alloc([128, 16, 128], BF16, "YlT")
    gw = A.alloc([128, S], BF16, "gw")
    bg = A.alloc([128, S], BF16, "bg")
    Hall = A.alloc([128, 17, 64], F32, "Hall")
    t_pp = [Tok("pp%d" % i) for i in range(NB)]
    t_H = Tok("H")
    pr = [[(A.alloc([128, 1 + W], F32, "pr"), Tok("pr")) for _ in range(2)] for _ in range(3)]

    def T_(shape, dt=F32, name="tmp"):
        return A.alloc(shape, dt, name), Tok(name)

    X1, t_X1 = T_([128, W], name="X1")
    X2, t_X2 = T_([128, W], name="X2")
    t1, t_t1 = X1, t_X1
    csx, t_csx = X1, t_X1
    bv, t_bv = X1, t_X1
    kk2, t_kk2 = X2, t_X2
    km, t_km = X2, t_X2
    rk, t_rk = X2, t_X2
    r_s, t_r = T_([128, W], name="r_s")
    k_s, t_k = T_([128, W], name="k_s")
    v_s, t_v = T_([128, W], name="v_s")
    sig, t_sig = T_([128, W], name="sig")
    a_s, t_a = T_([128, W], name="a_s")
    g_s, t_g = T_([128, W], name="g_s")
    cs, t_cs = T_([128, W], name="cs")
    Eex, t_Eex = T_([128, W], name="Eex")
    Eneg, t_Eneg = T_([128, W], name="Eneg")
    kk, t_kk = T_([128, W], name="kk")
    rn, t_rn = X2, t_X2
    BKh, t_BKh = T_([128, 2 * 256], BF16, name="BKh")
    ABT, t_ABT = T_([128, 4, 2, 128], BF16, name="ABT")
    AKT, t_AKT = T_([128, 4, 2, 128], BF16, name="AKT")
    u4 = lambda ap: ap.rearrange("p (u t) -> p u t", u=4)
    Lf = [T_([128, 512], BF16, name="L") for _ in range(2)]
    Pf = [T_([128, 512], BF16, name="Pm") for _ in range(2)]
    Lb = [(u4(a), t) for a, t in Lf]
    Pb = [(u4(a), t) for a, t in Pf]
    Zf_, t_Zf = T_([128, 512], F32, name="Zf")
    Zh_, t_Zh = T_([128, 512], BF16, name="Zh")
    Zf, Zh = u4(Zf_), u4(Zh_)
    v_bf, t_vbf = T_([128, W], BF16, name="v_bf")
    Ot = [T_([128, 512], F32, name="Ot") for _ in range(4)]
    v2 = lambda ap: ap.rearrange("p (c t) -> p c t", c=2)
    XS = []
    for i in range(2):
        d = {}
        d["AR"], d["t_AR"] = T_([128, 2 * 256], BF16, name="AR")
        d["BK"], d["t_BK"] = T_([128, 2 * 256], BF16, name="BK")
        d["TM"], d["t_TM"] = T_([128, 2, 4, 128], BF16, name="TM")
        d["Ein"], d["t_Ein"] = T_([128, W], name="Ein")
        d["gwb"], d["t_gwb"] = T_([128, W], BF16, name="gwb")
        d["bgb"], d["t_bgb"] = T_([128, W], BF16, name="bgb")
        d["AR4"] = d["AR"].rearrange("p (c k t) -> p c k t", c=2, k=2)
        d["BK4"] = d["BK"].rearrange("p (c k t) -> p c k t", c=2, k=2)
        XS.append(d)

    Yb, t_Yb = Ot[0]
    Yc, t_Yc = Ot[1]
    Ysq, t_Ysq = Ot[2]
    rstd, t_rstd = Ot[3]

    ring1 = [0]
    ring2 = [0]

    def psum1():
        bk = banks[ring1[0] % 4]
        ring1[0] += 1
        return bk

    zheld = [None]

    def psum2():
        bk = banks[4 + ring2[0] % 4]
        zheld[0] = 4 + ring2[0] % 4
        ring2[0] += 1
        return bk

    def psum2b():
        while 4 + ring2[0] % 4 == zheld[0]:
            ring2[0] += 1
        bk = banks[4 + ring2[0] % 4]
        ring2[0] += 1
        return bk

    def S1(p, b):
        X = XS[b % 2]
        AR, t_AR, BK, t_BK, TM, t_TM, Ein, t_Ein = X["AR"], X["t_AR"], X["BK"], X["t_BK"], X["TM"], X["t_TM"], X["Ein"], X["t_Ein"]
        AR4, BK4 = X["AR4"], X["BK4"]
        wset = wrkv[p % 2]
        if b == 0:
            for role in range(3):
                wb, twb = wset[role]
                cc = role * 4 + p
                P.dma(POOL, (lambda wb, cc: lambda e: e.dma_start(out=wb, in_=w_in_v[:, :, cc * 128:cc * 128 + 128]))(wb, cc),
                      "wrkv%d_%d" % (p % 2, role), writes=[twb])
            for role in range(3):
                P.memset(DVE, pr[role][0][0][:, 0:1], 0.0, [pr[role][0][1]])
        tsl = slice(b * W, b * W + W)
        hdeps = hT_all[b * 2:b * 2 + 2]
        outs = [(r_s, t_r), (k_s, t_k), (v_s, t_v)]
        for role in range(3):
            wb, twb = wset[role]
            cc = role * 4 + p
            ps, tp = psum1()
            for dc in range(8):
                P.mm(ps[:, 0:W], wb[:, dc, :], hT[:, dc, tsl], [twb] + hdeps, [tp], start=(dc == 0), stop=(dc == 7))
            yield
            prb, tprb = pr[role][b % 2]
            prn, tprn = pr[role][(b + 1) % 2]
            P.copy(ACT, prb[:, 1:1 + W], ps[:, 0:W], [tp], [tprb])
            P.copy(ACT, prn[:, 0:1], prb[:, W:W + 1], [tprb], [tprn])
            P.ts(DVE, t1, prb[:, 0:W], pcols[:, MU0 + cc:MU0 + cc + 1], ALU.mult, [tprb, tC], [t_t1])
            o, to = outs[role]
            P.stt(o, prb[:, 1:1 + W], omm[:, cc:cc + 1], t1, ALU.mult, ALU.add, [tprb, t_t1, tC], [to])
            yield
        sdep = [t_sh[b // 2]]
        psw, tpw = psum1()
        P.mm(psw[:, 0:W], lora[0:64, p * 128:p * 128 + 128], sh12[0:64, tsl], [tC] + sdep, [tpw])
        psa, tpa = psum1()
        P.mm(psa[:, 0:W], lora[64:128, p * 128:p * 128 + 128], sh12[64:128, tsl], [tC] + sdep, [tpa])
        psg, tpg = psum1()
        P.mm(psg[:, 0:W], lora[:, 512 + p * 128:512 + p * 128 + 128], sh13[:, tsl], [tC] + sdep, [tpg])
        yield
        P.act(sig, psw[:, 0:W], AF.Exp, [tpw, tC], [t_sig], bias=nwa[:, p:p + 1], scale=-1.0)
        P.act(a_s, psa[:, 0:W], AF.Exp, [tpa, tC], [t_a], bias=nwa[:, 4 + p:4 + p + 1], scale=-1.0)
        P.copy(ACT, g_s, psg[:, 0:W], [tpg], [t_g])
        yield
        P.act(sig, sig, AF.Ln, [t_sig, tC], [t_sig], bias=onec)
        P.act(a_s, a_s, AF.Ln, [t_a, tC], [t_a], bias=onec)
        P.act(sig, sig, AF.Exp, [t_sig], [t_sig], scale=-1.0)
        P.act(a_s, a_s, AF.Exp, [t_a], [t_a], scale=-1.0)
        yield
        P.op(DVE, lambda e: e.tensor_tensor_scan(out=cs, data0=rmask, data1=sig, initial=0.0, op0=ALU.mult,
                                                 op1=ALU.add), [t_sig, tC], [t_cs])
        P.tt(DVE, csx, cs, sig, ALU.subtract, [t_cs, t_sig], [t_csx])
        yield
        P.act(Ein, cs, AF.Exp, [t_cs], [t_Ein], scale=-CEXP)
        P.act(Eex, csx, AF.Exp, [t_csx], [t_Eex], scale=-CEXP)
        P.act(Eneg, cs, AF.Exp, [t_cs], [t_Eneg], scale=CEXP)
        yield
        P.act(kk2, k_s, AF.Square, [t_k, tC], [t_kk2], scale=pcols[:, KK0 + p:KK0 + p + 1])
        P.mm(psg[:, W:2 * W], BD1, kk2, [tC, t_kk2], [tpg])
        yield
        P.act(rn, psg[:, W:2 * W], AF.Ln, [tpg, tC], [t_rn], bias=tinyc)
        P.act(rn, rn, AF.Exp, [t_rn], [t_rn], scale=-0.5)
        P.stt(kk, k_s, pcols[:, KK0 + p:KK0 + p + 1], rn, ALU.mult, ALU.mult, [t_k, t_rn, tC], [t_kk])
        yield
        P.ts(DVE, km, a_s, pcols[:, KA0 + p:KA0 + p + 1], ALU.mult, [t_a, tC], [t_km], s2=omka[:, p:p + 1], op1=ALU.add)
        P.tt(DVE, k_s, k_s, km, ALU.mult, [t_k, t_km], [t_k])
        P.tt(DVE, bv, kk, a_s, ALU.mult, [t_kk, t_a], [t_bv])
        yield
        P.stt(AR4[:, :, 0, :], v2(kk), -1.0, v2(Eex), ALU.mult, ALU.mult, [t_kk, t_Eex], [t_AR])
        P.tt(DVE, AR4[:, :, 1, :], v2(r_s), v2(Ein), ALU.mult, [t_r, t_Ein], [t_AR])
        yield
        P.tt(DVE, BK4[:, :, 0, :], v2(bv), v2(Eneg), ALU.mult, [t_bv, t_Eneg], [t_BK])
        P.tt(DVE, BK4[:, :, 1, :], v2(k_s), v2(Eneg), ALU.mult, [t_k, t_Eneg], [t_BK])
        yield
        for c in range(2):
            P.ts(DVE, BKh[:, c * 256:c * 256 + 256], BK[:, c * 256:c * 256 + 256], Ein[:, c * 128 + 127:c * 128 + 128],
                 ALU.mult, [t_BK, t_Ein], [t_BKh])
        yield
        P.stt(rk, r_s, pcols[:, RK0 + p:RK0 + p + 1], k_s, ALU.mult, ALU.mult, [t_r, t_k, tC], [t_rk])
        psb, tpb = psum1()
        P.mm(psb[:, 0:W], BD1, rk, [tC, t_rk], [tpb])
        yield
        P.tt(DVE, rk, psb[:, 0:W], v_s, ALU.mult, [tpb, t_v], [t_rk])
        P.ts(DVE, X["gwb"], g_s, pcols[:, LNW0 + p:LNW0 + p + 1], ALU.mult, [t_g, tC], [X["t_gwb"]])
        P.stt(X["bgb"], rk, pcols[:, LNB0 + p:LNB0 + p + 1], g_s, ALU.add, ALU.mult, [t_rk, t_g, tC], [X["t_bgb"]])
        yield
        P.copy(POOL, v_bf, v_s, [t_v], [t_vbf])
        for c in range(2):
            pst, tpt = psum1()
            srcs = [(AR[:, c * 256:c * 256 + 128], t_AR), (BKh[:, c * 256:c * 256 + 128], t_BKh),
                    (BKh[:, c * 256 + 128:c * 256 + 256], t_BKh), (v_bf[:, c * 128:c * 128 + 128], t_vbf)]
            pstb = pst.bitcast(BF16)
            for j, (sap, stok) in enumerate(srcs):
                P.tr(pstb[:, 128 * j:128 * j + 128], sap, identb, [stok, tC], [tpt])
            P.copy(ACT, TM[:, c, :, :], pstb[:, 0:512].rearrange("p (j t) -> p j t", j=4), [tpt], [t_TM])
            yield

    def S2(p, b):
        X = XS[b % 2]
        AR, t_AR, BK, t_BK, TM, t_TM, Ein, t_Ein = X["AR"], X["t_AR"], X["BK"], X["t_BK"], X["TM"], X["t_TM"], X["Ein"], X["t_Ein"]
        AR4 = X["AR4"]
        tsl = slice(b * W, b * W + W)
        P.copy(POOL, gw[:, tsl], X["gwb"], [X["t_gwb"]], [t_pp[b]])
        P.copy(POOL, bg[:, tsl], X["bgb"], [X["t_bgb"]], [t_pp[b]])
        units = [(c, q) for c in range(2) for q in range(2)]
        for which, dstT, tdst in ((0, ABT, t_ABT), (1, AKT, t_AKT)):
            pss = [psum2(), psum2()]
            for c in range(2):
                for q in range(2):
                    pq = slice(64 * q, 64 * q + 64)
                    ps, tp = pss[q]
                    lhs = BK[pq, c * 256 + which * 128:c * 256 + which * 128 + 128]
                    P.mm(ps[:, c * 256:c * 256 + 256], lhs, AR[pq, c * 256:c * 256 + 256], [t_BK, t_AR], [tp])
            for q in range(2):
                ps, tp = pss[q]
                P.tt(DVE, dstT[:, q::2, :, :], ps.rearrange("p (u k t) -> p u k t", u=2, k=2),
                     maskA[:, 0:2, :, :], ALU.mult, [tp, tC], [tdst])
            yield
        L0, tL0 = Lb[0]
        for q in range(2):
            pq = slice(64 * q, 64 * q + 64)
            ps, tp = psum2()
            for c in range(2):
                P.mm(ps[:, c * 128:c * 128 + 128], AR[pq, c * 256:c * 256 + 128], BK[pq, c * 256:c * 256 + 128],
                     [t_AR, t_BK], [tp])
            P.tt(DVE, L0[:, q::2, :], ps[:, 0:256].rearrange("p (u t) -> p u t", u=2), maskL[:, 0:2, :], ALU.mult,
                 [tp, tC], [tL0])
            yield
        psZ, tpZ = psum2()
        first = True
        for u, (c, q) in enumerate(units):
            P.mm(psZ[:, u * 128:u * 128 + 64], identb, TM[:, c, 0, 64 * q:64 * q + 64], [tC, t_TM], [tpZ],
                 start=first, stop=False, sgc=True)
            first = False
            P.mm(psZ[:, u * 128 + 64:u * 128 + 128], AKT[:, u, 0, :], TM[:, c, 3, 64 * q:64 * q + 64], [t_AKT, t_TM], [tpZ],
                 start=False, stop=(u == 3), sgc=True)
        yield
        P.copy(ACT, Zh_, psZ, [tpZ], [t_Zh])
        yield
        Pk, tPk = ABT, t_ABT
        for lev in range(7):
            Lk, tLk = Lb[lev % 2]
            Ln, tLn = Lb[(lev + 1) % 2]
            Pn, tPn = Pb[lev % 2]
            pk = (lambda u: ABT[:, u, 0, :]) if lev == 0 else (lambda u, Pk=Pk: Pk[:, u, :])
            if lev < 6:
                ps1, tp1 = psum2b()
                ps2, tp2 = psum2b()
                for u in range(4):
                    P.mm(ps1[:, u * 128:u * 128 + 128], Lk[:, u, :], pk(u), [tLk, tPk], [tp1])
                    P.mm(ps2[:, u * 128:u * 128 + 128], pk(u), Lk[:, u, :], [tLk, tPk], [tp2])
            for u in range(4):
                P.mm(psZ[:, u * 128:u * 128 + 128], pk(u), Zh[:, u, :], [tPk, t_Zh], [tpZ], start=False, stop=(u == 3), sgc=True)
            yield
            if lev < 6:
                P.copy(DVE, Pn, ps1.rearrange("p (u t) -> p u t", u=4), [tp1], [tPn])
                P.copy(ACT, Ln, ps2.rearrange("p (u t) -> p u t", u=4), [tp2], [tLn])
                Pk, tPk = Pn, tPn
            P.copy(ACT, Zh_, psZ, [tpZ], [t_Zh])
            yield
        Z7, tZ7 = Zh, t_Zh
        psM, tpM = psum2b()
        psR, tpR = psum2b()
        for u, (c, q) in enumerate(units):
            pq = slice(64 * q, 64 * q + 64)
            tpos = (0, 64 * q)
            P.mm(psM[pq, c * 64:c * 64 + 64], Z7[:, u, 0:64], TM[:, c, 1, 64 * q:64 * q + 64], [tZ7, t_TM], [tpM], tp=tpos)
            P.mm(psM[pq, 128 + c * 64:128 + c * 64 + 64], TM[:, c, 1, 64 * q:64 * q + 64], Z7[:, u, 64:128],
                 [tZ7, t_TM], [tpM], start=True, stop=False, tp=tpos)
            P.mm(psM[pq, 128 + c * 64:128 + c * 64 + 64], TM[:, c, 2, 64 * q:64 * q + 64], TM[:, c, 3, 64 * q:64 * q + 64],
                 [t_TM], [tpM], start=False, stop=True, tp=tpos)
            P.mm(psR[pq, c * 128:c * 128 + 128], Z7[:, u, 0:64], ABT[:, u, 1, :], [tZ7, t_ABT], [tpR], tp=tpos)
            P.mm(psR[pq, 256 + c * 128:256 + c * 128 + 128], Z7[:, u, 64:128], ABT[:, u, 1, :], [tZ7, t_ABT], [tpR],
                 start=True, stop=False, tp=tpos)
            P.mm(psR[pq, 256 + c * 128:256 + c * 128 + 128], TM[:, c, 3, 64 * q:64 * q + 64], AKT[:, u, 1, :],
                 [t_TM, t_AKT], [tpR], start=False, stop=True, tp=tpos)
        yield
        for c in range(2):
            P.stt(MT_all[:, 2 * b + c, :], identpair, Ein[:, c * 128 + 127:c * 128 + 128], psM[:, c * 64:c * 64 + 64],
                  ALU.mult, ALU.add, [tpM, t_Ein, tC], [t_pp[b]])
        P.copy(ACT, Nn_all[:, 2 * b:2 * b + 2, :], psM[:, 128:256].rearrange("p (c v) -> p c v", c=2), [tpM], [t_pp[b]])
        P.tt(DVE, RhT_all[:, 2 * b:2 * b + 2, :], psR[:, 0:256].rearrange("p (c t) -> p c t", c=2), AR4[:, :, 1, :],
             ALU.add, [tpR, t_AR], [t_pp[b]])
        P.copy(ACT, YlT_all[:, 2 * b:2 * b + 2, :], psR[:, 256:512].rearrange("p (c t) -> p c t", c=2), [tpR], [t_pp[b]])
        yield
        if b == NB - 1:
            yield from S3(p)

    def S3(p):
        P.memset(DVE, Hall[:, 0, :], 0.0, [t_H])
        for c in range(16):
            ps, tp = psum2()
            for q in range(2):
                pq = slice(64 * q, 64 * q + 64)
                P.mm(ps[pq, 0:64], MT_all[pq, c, :], Hall[pq, c, :], [t_pp[c // 2], t_H], [tp], tp=(64 * q, 64 * q))
            P.tt(DVE, Hall[:, c + 1, :], ps[:, 0:64], Nn_all[:, c, :], ALU.add, [tp, t_pp[c // 2]], [t_H])
            yield
        for b4 in range(4):
            ps, tp = psum2()
            for cj in range(4):
                c = b4 * 4 + cj
                for q in range(2):
                    pq = slice(64 * q, 64 * q + 64)
                    P.mm(ps[pq, cj * 128:cj * 128 + 128], Hall[pq, c, :], RhT_all[pq, c, :], [t_H, t_pp[c // 2]], [tp],
                         tp=(64 * q, 64 * q))
            P.tt(DVE, u4(Yb), u4(ps), YlT_all[:, b4 * 4:b4 * 4 + 4, :], ALU.add,
                 [tp, t_pp[b4 * 2], t_pp[b4 * 2 + 1]], [t_Yb])
            if p == 0 and b4 == 0:
                dump("Y0", Yb, t_Yb)
            yield
            psm, tpm = psum2()
            P.mm(psm, BD64, Yb, [tC, t_Yb], [tpm])
            P.tt(DVE, Yc, Yb, psm, ALU.subtract, [t_Yb, tpm], [t_Yc])
            P.act(Ysq, Yc, AF.Square, [t_Yc], [t_Ysq])
            yield
            psv, tpv = psum2()
            P.mm(psv, BD64, Ysq, [tC, t_Ysq], [tpv])
            P.act(rstd, psv, AF.Ln, [tpv, tC], [t_rstd], bias=epsG)
            P.act(rstd, rstd, AF.Exp, [t_rstd], [t_rstd], scale=-0.5)
            yield
            sl = slice(b4 * 512, b4 * 512 + 512)
            P.tt(DVE, Yc, Yc, rstd, ALU.mult, [t_Yc, t_rstd], [t_Yc])
            P.tt(DVE, Yc, Yc, gw[:, sl], ALU.mult, [t_Yc, t_pp[b4 * 2], t_pp[b4 * 2 + 1]], [t_Yc])
            P.tt(DVE, yT[:, p, sl], Yc, bg[:, sl], ALU.add, [t_Yc, t_pp[b4 * 2], t_pp[b4 * 2 + 1]], [t_yT[p]])
            yield

    NBLK = 4 * NB
    if upto.startswith('B') and len(upto) == 3:
        NBLK = int(upto[2])
    i1 = 0
    f1 = 0
    f2 = 0
    g1 = None
    g2 = None
    while f2 < NBLK:
        if g1 is None and i1 < NBLK and i1 - f2 <= 1:
            g1 = S1(i1 // NB, i1 % NB)
            i1 += 1
        if g2 is None and f2 < f1:
            g2 = S2(f2 // NB, f2 % NB)
        if g1 is not None:
            try:
                next(g1)
            except StopIteration:
                g1 = None
                f1 += 1
        if g2 is not None:
            try:
                next(g2)
            except StopIteration:
                g2 = None
                f2 += 1
    dump("yT0", yT[:, 0, :], t_yT[0])
    P.barrier()
    A.reset(phase_mark)
    if upto == 'B':
        P.replay()
        return nc

    A.cur += 16 * D * 4
    wo, t_wo = T_([128, 8, D], BF16, "wo")
    wo_end = A.mark()
    A.reset(phase_mark)
    for half in range(2):
        P.dma(POOL, (lambda half: lambda e: e.dma_start(out=wo[:, :, half * 512:half * 512 + 512],
                                                        in_=w_out_v[:, :, half * 512:half * 512 + 512]))(half),
              "wo", writes=[t_wo])
    watt = [[(A.alloc([128, 8, 128], BF16, "watt"), Tok("watt")) for _ in range(3)] for _ in range(2)]
    QT, t_QT = T_([128, S], BF16, "QT")
    KT, t_KT = T_([128, S], BF16, "KT")
    Vpad = [T_([128, 16, 2, 128], BF16, "Vpad") for _ in range(2)]
    acc_o, t_acco = T_([128, S], F32, "acc_o")
    acc_d, t_accd = T_([128, S], F32, "acc_d")
    peT = [T_([128, 512], BF16, "peT") for _ in range(4)]
    otmp, t_otmp = T_([128, 512], F32, "otmp")
    osq, t_osq = T_([128, 512], F32, "osq")
    orst, t_orst = osq, t_osq
    VT, t_VT = T_([128, S], BF16, "VT")
    for vb, tvb in Vpad:
        P.memset(POOL, vb, 0.0, [tvb])

    def tokset(br, blk):
        if br == 0:
            return 128 * blk, 1
        if br == 1:
            r2, n2 = blk // 4, blk % 4
            return 512 * n2 + r2, 4
        return blk, 16

    def tsl_(br, blk):
        st, sp = tokset(br, blk)
        return slice(st, st + sp * 127 + 1, sp)

    def has_prev(br, blk):
        if br == 0:
            return blk >= 1
        if br == 1:
            return blk % 4 >= 1
        return False

    ringC = [0]

    def psumC():
        bk = banks[4 + ringC[0] % 4]
        ringC[0] += 1
        return bk

    vcount = 0
    pecount = 0
    gcount = 0
    for p in range(4):
        wset = watt[p % 2]
        for role in range(3):
            wb, twb = wset[role]
            cc = 14 + role * 4 + p
            P.dma(POOL, (lambda wb, cc: lambda e: e.dma_start(out=wb, in_=w_in_v[:, :, cc * 128:cc * 128 + 128]))(wb, cc),
                  "watt%d_%d" % (p % 2, role), writes=[twb])
        for role, (dst, tdst) in enumerate(((QT, t_QT), (KT, t_KT))):
            wb, twb = wset[role]
            for b4 in range(4):
                ps, tp = psumC()
                for dc in range(8):
                    P.mm(ps, wb[:, dc, :], hT[:, dc, b4 * 512:b4 * 512 + 512], [twb] + hT_all[b4 * 4:b4 * 4 + 4], [tp],
                         start=(dc == 0), stop=(dc == 7))
                P.act(dst[:, b4 * 512:b4 * 512 + 512], ps, AF.Copy, [tp], [tdst], scale=(0.125 if role == 0 else 1.0))
        wv, twv = wset[2]
        vbufs = {}

        for b4 in range(4):
            ps, tp = psumC()
            for dc in range(8):
                P.mm(ps, wv[:, dc, :], hT[:, dc, b4 * 512:b4 * 512 + 512], [twv] + hT_all[b4 * 4:b4 * 4 + 4], [tp],
                     start=(dc == 0), stop=(dc == 7))
            P.copy(DVE, VT[:, b4 * 512:b4 * 512 + 512], ps, [tp], [t_VT])

        def vproj(br):
            nonlocal vcount
            vb, tvb = Vpad[vcount % 2]
            vcount += 1
            vbufs[br] = (vb, tvb)
            for g4 in range(4):
                ps, tp = psumC()
                psb = ps.bitcast(BF16)
                for j in range(4):
                    blk = g4 * 4 + j
                    P.tr(psb[:, j * 128:j * 128 + 128], VT[:, tsl_(br, blk)], identb, [t_VT, tC], [tp])
                psv4 = psb[:, 0:512].rearrange("p (j q e) -> p j q e", j=4, q=2)
                for q in range(2):
                    P.copy(ACT if q == 0 else DVE, vb[:, g4 * 4:g4 * 4 + 4, q, 64 * q:64 * q + 64], psv4[:, :, q, :], [tp], [tvb])

        def Xs(br, g4, jp):
            nonlocal pecount
            js = (2 * jp, 2 * jp + 1)
            blks = [g4 * 4 + j for j in js]
            hps = [has_prev(br, blk) for blk in blks]
            mb = MB_TT if (hps[0] and hps[1]) else (MB_FT if hps[1] else MB_FF)
            pes = []
            pss = [psumC(), psumC()]
            for ji in range(2):
                blk = blks[ji]
                qs = tsl_(br, blk)
                if hps[ji]:
                    for q in range(2):
                        pq = slice(64 * q, 64 * q + 64)
                        psS, tpS = pss[q]
                        P.mm(psS[:, ji * 256:ji * 256 + 128], KT[pq, tsl_(br, blk - 1)], QT[pq, qs], [t_KT, t_QT], [tpS])
                for q in range(2):
                    pq = slice(64 * q, 64 * q + 64)
                    psS, tpS = pss[q]
                    P.mm(psS[:, ji * 256 + 128:ji * 256 + 256], KT[pq, qs], QT[pq, qs], [t_KT, t_QT], [tpS])
            for q in range(2):
                psS, tpS = pss[q]
                pe, tpe = peT[pecount % 4]
                pecount += 1
                if hps[0] and hps[1]:
                    v = lambda a: a
                elif hps[1]:
                    v = lambda a: a[:, 128:512]
                else:
                    v = lambda a: a.rearrange("p (j k t) -> p j k t", j=2, k=2)[:, :, 1, :]
                P.act(v(pe), v(psS), AF.Exp, [tpS], [tpe])
                P.tt(DVE, v(pe), v(pe), v(mb), ALU.mult, [tpe, tC], [tpe])
                pes.append((pe, tpe))
            return (js, blks, hps, pes)

        gstate = {}

        def Ys(br, g4, jp, xs):
            nonlocal gcount
            js, blks, hps, pes = xs
            vb, tvb = vbufs[br]
            if jp == 0:
                pb = (gcount % 2) * 2
                gcount += 1
                gstate[(br, g4)] = (banks[pb], banks[pb + 1])
            (psO, tpO), (psD, tpD) = gstate[(br, g4)]
            for ji in range(2):
                j = js[ji]
                blk = blks[ji]
                mms = []
                for q in range(2):
                    pe, tpe = pes[q]
                    if hps[ji]:
                        mms.append((q, blk - 1, pe[:, ji * 256:ji * 256 + 128], tpe))
                    mms.append((q, blk, pe[:, ji * 256 + 128:ji * 256 + 256], tpe))
                for i, (q, kb, rhs, tpe) in enumerate(mms):
                    P.mm(psO[:, j * 128:j * 128 + 128], vb[:, kb, q, :], rhs, [tvb, tpe], [tpO],
                         start=(i == 0), stop=(i == len(mms) - 1))
                for i, (q, kb, rhs, tpe) in enumerate(mms):
                    P.mm(psD[:, j * 128:j * 128 + 128], onespad[:, q, :], rhs, [tC, tpe], [tpD],
                         start=(i == 0), stop=(i == len(mms) - 1))
            if jp == 1:
                if br == 0:
                    oa = acc_o[:, g4 * 512:g4 * 512 + 512]
                    da = acc_d[:, g4 * 512:g4 * 512 + 512]
                    P.copy(ACT, oa, psO, [tpO], [t_acco])
                    P.copy(DVE, da, psD, [tpD], [t_accd])
                else:
                    if br == 1:
                        view = lambda a: a.rearrange("p (n i r) -> p r n i", n=4, r=4)[:, g4, :, :]
                    else:
                        view = lambda a: a.rearrange("p (i r) -> p r i", r=16)[:, g4 * 4:g4 * 4 + 4, :]
                    P.tt(DVE, view(acc_o), view(acc_o), psO.rearrange("p (j i) -> p j i", j=4), ALU.add, [tpO, t_acco], [t_acco])
                    P.tt(DVE, view(acc_d), view(acc_d), psD.rearrange("p (j i) -> p j i", j=4), ALU.add, [tpD, t_accd], [t_accd])

        U = [(br, g4, jp) for br in range(3) for g4 in range(4) for jp in range(2)]
        vproj(0)
        xs_next = Xs(*U[0])
        for k in range(len(U)):
            xs_cur = xs_next
            if k + 1 < len(U):
                if U[k + 1][0] != U[k][0]:
                    vproj(U[k + 1][0])
                xs_next = Xs(*U[k + 1])
            Ys(*U[k], xs_cur)
        for b4 in range(4):
            sl = slice(b4 * 512, b4 * 512 + 512)
            P.act(otmp, acc_d[:, sl], AF.Ln, [t_accd], [t_otmp])
            P.act(otmp, otmp, AF.Exp, [t_otmp], [t_otmp], scale=-1.0)
            P.tt(DVE, otmp, otmp, acc_o[:, sl], ALU.mult, [t_otmp, t_acco], [t_otmp])
            P.act(osq, otmp, AF.Square, [t_otmp], [t_osq])
            ps, tp = psumC()
            P.mm(ps, BD64, osq, [tC, t_osq], [tp])
            P.act(orst, ps, AF.Ln, [tp, tC], [t_orst], bias=epsN)
            P.act(orst, orst, AF.Exp, [t_orst], [t_orst], scale=-0.5)
            P.stt(yT[:, 4 + p, sl], otmp, pcols[:, OG0 + p:OG0 + p + 1], orst, ALU.mult, ALU.mult, [t_otmp, t_orst, tC],
                  [t_yT[4 + p]])
    dump("yT4", yT[:, 4, :], t_yT[4])
    assert A.mark() <= phase_mark + 16 * D * 4, "phase C buffers overlap the prefetched w_out"
    P.barrier()
    A.reset(phase_mark)
    if upto == 'C':
        P.replay()
        return nc

    x1, t_x1d = T_([128, 16, D], F32, "x1")
    t_x1 = [Tok("x1_%d" % i) for i in range(16)]
    x1_end = A.mark()
    A.reset(wo_end)
    xr = [T_([128, D], F32, "xr") for _ in range(2)]
    for tt_ in range(16):
        xa, txa = xr[tt_ % 2]
        P.dma(SP, (lambda xa, i: lambda e: e.dma_start(out=xa, in_=x_v[:, i, :]))(xa, tt_), "xr%d" % (tt_ % 2), writes=[txa])
        for ch in range(2):
            ps, tp = psum()
            for cc in range(8):
                P.mm(ps, yT[:, cc, tt_ * 128:tt_ * 128 + 128], wo[:, cc, ch * 512:ch * 512 + 512], [t_yT[cc], t_wo], [tp],
                     start=(cc == 0), stop=(cc == 7))
            P.tt(DVE, x1[:, tt_, ch * 512:ch * 512 + 512], ps, xa[:, ch * 512:ch * 512 + 512], ALU.add, [tp, txa], [t_x1[tt_]])
    dump("x1", x1[:, 0, :], t_x1[0])
    P.barrier()
    A.reset(persist_mark)
    h2Ts = [T_([128, 8, 512], BF16, "h2T") for _ in range(2)]
    aT, t_aTd = T_([128, NFF, 512], BF16, "aT")
    t_aT = [Tok("aT%d" % i) for i in range(NFF)]
    scrD = [(A.alloc([128, D], F32, "hn"), Tok("hn"), A.alloc([128, D], BF16, "sq"), Tok("sq")) for _ in range(1)]
    wgu = [[T_([128, 8, 128], BF16, "wgu") for _ in range(2)] for _ in range(2)]
    sg = [T_([128, 512], BF16, "sg") for _ in range(2)]
    xo = [T_([128, D], F32, "xo") for _ in range(2)]
    assert A.mark() <= phase_mark, "overlay exceeds dead hT/yT region"
    A.reset(x1_end)
    wd, t_wdd = T_([128, NFF, D], BF16, "wd")
    t_wd = [Tok("wd%d" % j) for j in range(NFF)]
    ncol = 16

    def norm_q(qt):
        nonlocal ncol
        h2T, t_h2 = h2Ts[qt % 2]
        for tl in range(4):
            tt_ = qt * 4 + tl
            norm_p1(x1[:, tt_, :], t_x1[tt_], g2bc, scrD[0], ncol)
            ncol += 1
            norm_p2(scrD[0], h2T, tl * 128, t_h2, psum)

    gucount = 0
    norm_q(0)
    for qt in range(4):
        h2T, t_h2 = h2Ts[qt % 2]
        for j in range(NFF):
            wg_, twg = wgu[gucount % 2][0]
            wu_, twu = wgu[gucount % 2][1]
            key = gucount % 2
            gucount += 1
            P.dma(POOL, (lambda wg_, j: lambda e: e.dma_start(out=wg_, in_=w_gate_v[:, :, j * 128:j * 128 + 128]))(wg_, j),
                  "wg%d" % key, writes=[twg])
            P.dma(POOL, (lambda wu_, j: lambda e: e.dma_start(out=wu_, in_=w_up_v[:, :, j * 128:j * 128 + 128]))(wu_, j),
                  "wu%d" % key, writes=[twu])
            if qt == 0:
                P.dma(POOL, (lambda j: lambda e: e.dma_start(out=wd[:, j, :], in_=w_down_v[:, j, :]))(j), "wd%d" % j,
                      writes=[t_wd[j]])
            psg, tpg = psum()
            psu, tpu = psum()
            for dc in range(8):
                P.mm(psg, wg_[:, dc, :], h2T[:, dc, :], [twg, t_h2], [tpg], start=(dc == 0), stop=(dc == 7))
            for dc in range(8):
                P.mm(psu, wu_[:, dc, :], h2T[:, dc, :], [twu, t_h2], [tpu], start=(dc == 0), stop=(dc == 7))
            sgb, tsg = sg[j % 2]
            P.act(sgb, psg, AF.Silu, [tpg], [tsg])
            P.tt(DVE, aT[:, j, :], sgb, psu, ALU.mult, [tsg, tpu], [t_aT[j]])
        if qt + 1 < 4:
            norm_q(qt + 1)
        for tl in range(4):
            tt_ = qt * 4 + tl
            xob, txo = xo[tt_ % 2]
            for ch in range(2):
                ps, tp = psum()
                for j in range(NFF):
                    P.mm(ps, aT[:, j, tl * 128:tl * 128 + 128], wd[:, j, ch * 512:ch * 512 + 512], [t_aT[j], t_wd[j]], [tp],
                         start=(j == 0), stop=(j == NFF - 1))
                P.tt(DVE, xob[:, ch * 512:ch * 512 + 512], ps, x1[:, tt_, ch * 512:ch * 512 + 512], ALU.add,
                     [tp, t_x1[tt_]], [txo])
            hn_, t_hn_, sc, tsc = scrD[0]
            tS_ = tSs[ncol]
            P.act(sc, xob, AF.Square, [txo], [tsc, tS_], accum=ssq[:, ncol:ncol + 1])
            P.act(ssq[:, ncol:ncol + 1], ssq[:, ncol:ncol + 1], AF.Ln, [tS_, tC], [tS_], bias=epsN, scale=1.0 / D)
            P.act(ssq[:, ncol:ncol + 1], ssq[:, ncol:ncol + 1], AF.Exp, [tS_], [tS_], scale=-0.5)
            P.stt(xob, xob, ssq[:, ncol:ncol + 1], gfbc, ALU.mult, ALU.mult, [txo, tS_, tC], [txo])
            ncol += 1
            P.dma(SP, (lambda xob, i: lambda e: e.dma_start(out=out_v[:, i, :], in_=xob))(xob, tt_), "xo%d" % (tt_ % 2),
                  reads=[txo], is_out=True)
    P.replay()
    return nc


_NC = {}


def _host_layout(inp):
    f = lambda a: np.ascontiguousarray(np.asarray(a, dtype=np.float32))
    col = lambda v: f(v).reshape(-1, 128).T
    pcols = np.concatenate([
        col(inp["mu_shift"][0]), col(inp["decay_w0"][0]), col(inp["iclr_a0"][0]), col(inp["k_k"][0]),
        col(inp["k_a"][0]), col(inp["r_k"][0].reshape(-1)), col(inp["ln_x_w"][0]), col(inp["ln_x_b"][0]),
        col(inp["attn_out_g"][0])], axis=1)
    assert pcols.shape == (128, NPC)
    gbc = np.stack([np.broadcast_to(f(inp["mix_norm_g"][0])[None, :], (128, D)),
                    np.broadcast_to(f(inp["ffn_norm_g"][0])[None, :], (128, D)),
                    np.broadcast_to(f(inp["final_norm_g"])[None, :], (128, D))], axis=0)
    lora = np.concatenate([np.concatenate([f(inp["decay_w2"][0]), f(inp["iclr_a2"][0])], axis=0),
                           f(inp["gate_g2"][0])], axis=1)
    shared = {
        "w_in": f(inp["w_in"][0]), "w_out": f(inp["w_out"][0]), "w_gate": f(inp["w_gate"][0]),
        "w_up": f(inp["w_up"][0]), "w_down": f(inp["w_down"][0]),
        "pcols": f(pcols), "gbc": f(gbc), "lora": f(lora),
    }
    return shared


def kernel(**inputs):
    x = np.asarray(inputs["x"], dtype=np.float32)
    shared = _host_layout(inputs)
    if "nc" not in _NC:
        _NC["nc"] = build()
    nc = _NC["nc"]
    in_maps = []
    for b in range(8):
        m = dict(shared)
        m["x"] = np.ascontiguousarray(x[b])
        in_maps.append(m)
    res = run_bass_kernel_spmd(nc, in_maps, core_ids=list(range(8)))
    return np.stack([r["out"] for r in res.results], axis=0).astype(np.float32)
```

```python
import contextlib
import numpy as np
import concourse.bass as bass
import concourse.mybir as mybir
from concourse.bass_utils import run_bass_kernel_spmd

F32 = mybir.dt.float32
BF16 = mybir.dt.bfloat16
AF = mybir.ActivationFunctionType
ALU = mybir.AluOpType

PE, ACT, DVE, POOL, SP = "tensor", "scalar", "vector", "gpsimd", "sync"
ENGINES = (PE, ACT, DVE, POOL, SP)

S = 2048
D = 1024
DFF = 2816
NFF = 22
INC = 3328
CEXP = float(np.exp(-0.5))
NEG = -30000.0
MU0, W00, A00, KK0, KA0, RK0, LNW0, LNB0, OG0, NPC = 0, 14, 18, 22, 26, 30, 34, 38, 42, 46

DEBUG = {}


class Tok:
    __slots__ = ("w", "r", "name")

    def __init__(self, name=""):
        self.w = None
        self.r = {}
        self.name = name


class Prog:
    def __init__(self, nc):
        self.nc = nc
        self.streams = {e: [] for e in ENGINES}
        self.seq = {e: 0 for e in ENGINES}
        self.dma_cnt = {}
        self.out_dma = []
        self.needed = {e: set() for e in ENGINES}
        self.pending = {e: None for e in ENGINES}
        self.nuniq = 0

    def _gather(self, eng, reads, writes):
        deps = set()
        for t in reads:
            if t.w is not None:
                deps.add(t.w)
        for t in writes:
            if t.w is not None:
                deps.add(t.w)
            deps.update(t.r.values())
        if self.pending[eng] is not None:
            deps.update(self.pending[eng])
            self.pending[eng] = None
        out = []
        for d in deps:
            if d[0] == 'e' and d[1] == PE and eng == PE:
                continue
            out.append(d)
            if d[0] == 'e':
                self.needed[d[1]].add(d[2])
        return out

    def _mark(self, me, reads, writes):
        key = (me[0], me[1])
        for t in reads:
            t.r[key] = me
        for t in writes:
            t.w = me
            t.r = {}

    def op(self, eng, fn, reads=(), writes=()):
        deps = self._gather(eng, reads, writes)
        self.seq[eng] += 1
        me = ('e', eng, self.seq[eng])
        self._mark(me, reads, writes)
        self.streams[eng].append((deps, fn, me))
        return me

    def dma(self, eng, fn, semkey, reads=(), writes=(), is_out=False):
        deps = self._gather(eng, reads, writes)
        self.dma_cnt[semkey] = self.dma_cnt.get(semkey, 0) + 1
        me = ('d', semkey, self.dma_cnt[semkey])
        self._mark(me, reads, writes)
        self.streams[eng].append((deps, fn, me))
        if is_out:
            self.out_dma.append(me)
        return me

    def barrier(self):
        deps = set()
        for e in ENGINES:
            if e != SP and self.seq[e] > 0:
                deps.add(('e', e, self.seq[e]))
        for k, c in self.dma_cnt.items():
            deps.add(('d', k, c))
        for e in ENGINES:
            cur = self.pending[e] or set()
            self.pending[e] = set(cur) | deps

    def replay(self):
        nc = self.nc
        ranks = {}
        for e in ENGINES:
            srt = sorted(self.needed[e])
            ranks[e] = {s: i + 1 for i, s in enumerate(srt)}
        with contextlib.ExitStack() as st:
            esem = {e: st.enter_context(nc.semaphore("s_" + e)) for e in ENGINES}
            dsem = {k: st.enter_context(nc.semaphore("d_%s" % (k,))) for k in self.dma_cnt}
            block = st.enter_context(nc.Block())

            def resolve(d):
                if d[0] == 'e':
                    return esem[d[1]], ranks[d[1]][d[2]]
                return dsem[d[1]], 16 * d[2]

            def run(ename, handle):
                waited = {}
                for deps, fn, me in self.streams[ename]:
                    for d in deps:
                        sem, val = resolve(d)
                        k = (d[0], d[1])
                        if waited.get(k, 0) < val:
                            handle.wait_ge(sem, val)
                            waited[k] = val
                    inst = fn(handle)
                    if me[0] == 'e':
                        if me[2] in ranks[ename]:
                            inst.then_inc(esem[ename], 1)
                    else:
                        inst.then_inc(dsem[me[1]], 16)
                if ename == SP:
                    for d in self.out_dma:
                        sem, val = resolve(d)
                        handle.wait_ge(sem, val)

            @block.tensor
            def _(h):
                run(PE, h)

            @block.scalar
            def _(h):
                run(ACT, h)

            @block.vector
            def _(h):
                run(DVE, h)

            @block.gpsimd
            def _(h):
                run(POOL, h)

            @block.sync
            def _(h):
                run(SP, h)

    def mm(self, out, lhsT, rhs, R, W, start=True, stop=True, tp=None, sgc=False):
        if sgc:
            return self.op(PE, lambda e: e.matmul(out, lhsT=lhsT, rhs=rhs, start=start, stop=stop, skip_group_check=True), R, W)
        if tp is None:
            return self.op(PE, lambda e: e.matmul(out, lhsT=lhsT, rhs=rhs, start=start, stop=stop), R, W)
        return self.op(PE, lambda e: e.matmul(out, lhsT=lhsT, rhs=rhs, start=start, stop=stop, tile_position=tp), R, W)

    def tr(self, out, in_, ident, R, W):
        return self.op(PE, lambda e: e.transpose(out=out, in_=in_, identity=ident), R, W)

    def act(self, out, in_, func, R, W, bias=None, scale=None, accum=None):
        kw = {}
        if bias is not None:
            kw["bias"] = bias
        if scale is not None:
            kw["scale"] = scale
        if accum is not None:
            kw["accum_out"] = accum
        return self.op(ACT, lambda e: e.activation(out=out, in_=in_, func=func, **kw), R, W)

    def tt(self, eng, out, a, b, op, R, W):
        return self.op(eng, lambda e: e.tensor_tensor(out=out, in0=a, in1=b, op=op), R, W)

    def ts(self, eng, out, a, s1, op0, R, W, s2=None, op1=None):
        if op1 is None:
            return self.op(eng, lambda e: e.tensor_scalar(out=out, in0=a, scalar1=s1, scalar2=None, op0=op0), R, W)
        return self.op(eng, lambda e: e.tensor_scalar(out=out, in0=a, scalar1=s1, scalar2=s2, op0=op0, op1=op1), R, W)

    def stt(self, out, a, s, b, op0, op1, R, W):
        return self.op(DVE, lambda e: e.scalar_tensor_tensor(out=out, in0=a, scalar=s, in1=b, op0=op0, op1=op1), R, W)

    def copy(self, eng, out, in_, R, W):
        if eng == ACT:
            return self.op(ACT, lambda e: e.copy(out=out, in_=in_), R, W)
        return self.op(eng, lambda e: e.tensor_copy(out=out, in_=in_), R, W)

    def memset(self, eng, ap, val, W):
        return self.op(eng, lambda e: e.memset(ap, val), (), W)

    def recip(self, out, in_, R, W):
        return self.op(DVE, lambda e: e.reciprocal(out=out, in_=in_), R, W)


class Arena:
    def __init__(self, nc, base=16640, top=229000):
        self.nc, self.base, self.top, self.cur, self.n = nc, base, top, base, 0

    def alloc(self, shape, dtype, name="t"):
        nbytes = int(np.prod(shape[1:])) * (4 if dtype == F32 else 2)
        nbytes = (nbytes + 63) // 64 * 64
        if self.cur + nbytes > self.top:
            raise RuntimeError("arena overflow %s %d" % (name, self.cur + nbytes - self.top))
        self.n += 1
        t = self.nc.alloc_sbuf_tensor_at("%s_%d" % (name, self.n), list(shape), dtype, offset=self.cur)
        self.cur += nbytes
        return t.ap()

    def mark(self):
        return self.cur

    def reset(self, m):
        self.cur = m


def build(dbg=None, upto='Z'):
    nc = bass.Bass("TRN2", target_bir_lowering=False)
    P = Prog(nc)
    A = Arena(nc)

    def dram(name, shape, kind="ExternalInput"):
        return nc.dram_tensor(name, list(shape), F32, kind=kind).ap()

    x_d = dram("x", [S, D])
    w_in_d = dram("w_in", [D, INC])
    w_out_d = dram("w_out", [D, D])
    w_gate_d = dram("w_gate", [D, DFF])
    w_up_d = dram("w_up", [D, DFF])
    w_down_d = dram("w_down", [DFF, D])
    pcols_d = dram("pcols", [128, NPC])
    gbc_d = dram("gbc", [3, 128, D])
    lora_d = dram("lora", [128, 1024])
    out_d = dram("out", [S, D], kind="ExternalOutput")
    dbg_d = {}
    if dbg:
        for k, shp in dbg.items():
            dt_ = BF16 if k in ("hT", "yT0", "yT4") else F32
            dbg_d[k] = nc.dram_tensor("dbg_" + k, list(shp), dt_, kind="ExternalOutput").ap()

    x_v = x_d.rearrange("(t p) d -> p t d", p=128)
    out_v = out_d.rearrange("(t p) d -> p t d", p=128)
    w_in_v = w_in_d.rearrange("(c p) n -> p c n", p=128)
    w_out_v = w_out_d.rearrange("(c p) n -> p c n", p=128)
    w_gate_v = w_gate_d.rearrange("(c p) n -> p c n", p=128)
    w_up_v = w_up_d.rearrange("(c p) n -> p c n", p=128)
    w_down_v = w_down_d.rearrange("(c p) n -> p c n", p=128)

    banks = [(nc.alloc_psum_tensor("ps%d" % i, [128, 512], F32).ap(), Tok("ps%d" % i)) for i in range(8)]
    bank_i = [0]

    def psum():
        b = banks[bank_i[0] % 8]
        bank_i[0] += 1
        return b

    dbg_dumps = []

    def dump(key, ap, tok):
        if dbg and key in dbg_d:
            P.dma(SP, lambda e: e.dma_start(out=dbg_d[key], in_=ap), "dbg_" + key, reads=[tok], is_out=True)

    tC = Tok("consts")
    pcols = A.alloc([128, NPC], F32, "pcols")
    omm = A.alloc([128, 14], F32, "omm")
    omka = A.alloc([128, 4], F32, "omka")
    g1bc = A.alloc([128, D], F32, "g1bc")
    g2bc = A.alloc([128, D], F32, "g2bc")
    gfbc = A.alloc([128, D], F32, "gfbc")
    lora = A.alloc([128, 1024], F32, "lora")
    ones = A.alloc([128, 128], F32, "ones")
    zeros = A.alloc([128, 128], F32, "zeros")
    ident = A.alloc([128, 128], F32, "ident")
    identb = A.alloc([128, 128], BF16, "identb")
    identpair = A.alloc([128, 64], F32, "identpair")
    BD1 = A.alloc([128, 128], F32, "bd1")
    BD64 = A.alloc([128, 128], F32, "bd64")
    maskA = A.alloc([128, 2, 2, 128], F32, "maskA")
    maskL = A.alloc([128, 2, 128], F32, "maskL")
    rmask = A.alloc([128, 256], F32, "rmask")
    MB_TT = A.alloc([128, 512], BF16, "MB_TT")
    MB_FT = A.alloc([128, 512], BF16, "MB_FT")
    MB_FF = A.alloc([128, 512], BF16, "MB_FF")
    mtmp = A.alloc([128, 256], F32, "mtmp")
    onespad = A.alloc([128, 2, 128], BF16, "onespad")
    nwa = A.alloc([128, 8], F32, "nwa")
    onec = A.alloc([128, 1], F32, "onec")
    tinyc = A.alloc([128, 1], F32, "tinyc")
    epsN = A.alloc([128, 1], F32, "epsN")
    epsG = A.alloc([128, 1], F32, "epsG")
    ssq = A.alloc([128, 64], F32, "ssq")
    tS = Tok("ssq")

    P.dma(SP, lambda e: e.dma_start(out=pcols, in_=pcols_d), "consts", writes=[tC])
    P.dma(SP, lambda e: e.dma_start(out=g1bc, in_=gbc_d[0]), "consts", writes=[tC])
    P.dma(SP, lambda e: e.dma_start(out=g2bc, in_=gbc_d[1]), "consts", writes=[tC])
    P.dma(SP, lambda e: e.dma_start(out=gfbc, in_=gbc_d[2]), "consts", writes=[tC])
    P.dma(SP, lambda e: e.dma_start(out=lora, in_=lora_d), "consts", writes=[tC])
    P.memset(POOL, ones, 1.0, [tC])
    P.memset(POOL, zeros, 0.0, [tC])
    P.memset(POOL, epsN, 1e-6, [tC])
    P.memset(POOL, epsG, 64e-5, [tC])
    P.memset(POOL, BD1, 0.0, [tC])
    P.memset(POOL, BD64, 0.0, [tC])
    P.memset(POOL, onespad, 0.0, [tC])
    for q in range(2):
        P.memset(POOL, BD1[64 * q:64 * q + 64, 64 * q:64 * q + 64], 1.0, [tC])
        P.memset(POOL, BD64[64 * q:64 * q + 64, 64 * q:64 * q + 64], 1.0 / 64, [tC])
        P.memset(POOL, onespad[:, q, 64 * q:64 * q + 64], 1.0, [tC])
    P.memset(POOL, rmask, 1.0, [tC])
    P.memset(POOL, rmask[:, 0:1], 0.0, [tC])
    P.memset(POOL, rmask[:, 128:129], 0.0, [tC])

    def asel(out, in_, cmp, fill, base, cm, step):
        P.op(POOL, lambda e: e.affine_select(out=out, in_=in_, pattern=[[step, 128]], compare_op=cmp, fill=fill,
                                             base=base, channel_multiplier=cm), [tC], [tC])

    asel(ident, ones, ALU.is_equal, 0.0, 0, -1, 1)
    for u in range(2):
        asel(maskA[:, u, 0, :], ones, ALU.is_ge, 0.0, -1, -1, 1)
        asel(maskA[:, u, 1, :], ones, ALU.is_ge, 0.0, 0, -1, 1)
        asel(maskL[:, u, :], ones, ALU.is_ge, 0.0, -1, 1, -1)
    asel(mtmp[:, 0:128], ones, ALU.is_ge, 0.0, 0, 1, -1)
    asel(mtmp[:, 128:256], ones, ALU.is_ge, 0.0, 0, -1, 1)
    for h2 in range(2):
        P.copy(POOL, MB_TT[:, 256 * h2:256 * h2 + 256], mtmp, [tC], [tC])
        P.memset(POOL, MB_FF[:, 256 * h2:256 * h2 + 128], 0.0, [tC])
        P.copy(POOL, MB_FF[:, 256 * h2 + 128:256 * h2 + 256], mtmp[:, 128:256], [tC], [tC])
    P.memset(POOL, MB_FT[:, 0:128], 0.0, [tC])
    P.copy(POOL, MB_FT[:, 128:256], mtmp[:, 128:256], [tC], [tC])
    P.copy(POOL, MB_FT[:, 256:512], mtmp, [tC], [tC])
    P.copy(POOL, identb, ident, [tC], [tC])
    P.tt(POOL, identpair, ident[:, 0:64], ident[:, 64:128], ALU.add, [tC], [tC])
    P.ts(DVE, omm, pcols[:, MU0:MU0 + 14], -1.0, ALU.mult, [tC], [tC], s2=1.0, op1=ALU.add)
    P.ts(DVE, omka, pcols[:, KA0:KA0 + 4], -1.0, ALU.mult, [tC], [tC], s2=1.0, op1=ALU.add)
    P.ts(DVE, nwa, pcols[:, W00:W00 + 8], -1.0, ALU.mult, [tC], [tC])
    P.memset(DVE, onec, 1.0, [tC])
    P.memset(DVE, tinyc, 1e-24, [tC])

    persist_mark = A.mark()

    hT = A.alloc([128, 8, S], BF16, "hT")
    yT = A.alloc([128, 8, S], BF16, "yT")
    t_hT = [Tok("hT%d" % i) for i in range(16)]
    t_yT = [Tok("yT%d" % i) for i in range(8)]
    phase_mark = A.mark()

    tSs = [Tok("ssq%d" % i) for i in range(64)]

    def norm_p1(src_ap, src_tok, gbc, scr, ncol):
        hn, t_hn, sq, t_sq = scr
        tS_ = tSs[ncol]
        P.act(sq, src_ap, AF.Square, [src_tok], [t_sq, tS_], accum=ssq[:, ncol:ncol + 1])
        P.act(ssq[:, ncol:ncol + 1], ssq[:, ncol:ncol + 1], AF.Ln, [tS_, tC], [tS_], bias=epsN, scale=1.0 / D)
        P.act(ssq[:, ncol:ncol + 1], ssq[:, ncol:ncol + 1], AF.Exp, [tS_], [tS_], scale=-0.5)
        P.stt(hn, src_ap, ssq[:, ncol:ncol + 1], gbc, ALU.mult, ALU.mult, [src_tok, tS_, tC], [t_hn])

    def norm_p2(scr, dstT, dst_col0, dst_tok, psfn):
        hn, t_hn, sq, t_sq = scr
        for half in range(2):
            ps, tp = psfn()
            for j in range(4):
                dc = half * 4 + j
                P.tr(ps[:, 128 * j:128 * j + 128], hn[:, 128 * dc:128 * dc + 128], ident, [t_hn, tC], [tp])
            P.copy(ACT if half == 0 else DVE, dstT[:, half * 4:half * 4 + 4, dst_col0:dst_col0 + 128],
                   ps.rearrange("p (j t) -> p j t", j=4), [tp], [dst_tok])

    NXT = 4
    NSC = 3
    xt = [(A.alloc([128, D], F32, "xt"), Tok("xt")) for _ in range(NXT)]
    scrA = [(A.alloc([128, D], F32, "hn"), Tok("hn"), A.alloc([128, D], BF16, "sq"), Tok("sq")) for _ in range(NSC)]
    def loadx(tt_):
        xa, txa = xt[tt_ % NXT]
        P.dma(SP, (lambda xa, i: lambda e: e.dma_start(out=xa, in_=x_v[:, i, :]))(xa, tt_), "xt%d" % (tt_ % NXT),
              writes=[txa])
        return xa, txa

    xq = {}
    for tt_ in range(min(NXT, 16)):
        xq[tt_] = loadx(tt_)
    for tt_ in range(2):
        norm_p1(xq[tt_][0], xq[tt_][1], g1bc, scrA[tt_ % NSC], tt_)
    for tt_ in range(16):
        if tt_ + 2 < 16:
            if tt_ + 2 not in xq:
                xq[tt_ + 2] = loadx(tt_ + 2)
            xa, txa = xq[tt_ + 2]
            norm_p1(xa, txa, g1bc, scrA[(tt_ + 2) % NSC], tt_ + 2)
        norm_p2(scrA[tt_ % NSC], hT, tt_ * 128, t_hT[tt_], psum)
    dump("hT", hT[:, 0, :], t_hT[15])
    P.barrier()
    A.reset(phase_mark)
    if upto == 'A':
        P.replay()
        return nc

    hT_all = t_hT
    W = 256
    NB = S // W
    sh12 = A.alloc([128, S], F32, "sh12")
    sh13 = A.alloc([128, S], F32, "sh13")
    t_sh = [Tok("sh%d" % i) for i in range(4)]
    mB0 = A.mark()
    wsh = [(A.alloc([128, 8, 128], BF16, "wsh"), Tok("wsh")) for _ in range(2)]
    prsb = [(A.alloc([128, 513], F32, "prs"), Tok("prs")) for _ in range(2)]
    t1sb = [(A.alloc([128, 512], F32, "t1s"), Tok("t1s")) for _ in range(2)]
    for ci, cc in enumerate((12, 13)):
        wb, twb = wsh[ci]
        P.dma(POOL, (lambda wb, cc: lambda e: e.dma_start(out=wb, in_=w_in_v[:, :, cc * 128:cc * 128 + 128]))(wb, cc),
              "wsh%d" % ci, writes=[twb])
    for ci, cc in enumerate((12, 13)):
        wb, twb = wsh[ci]
        P.memset(DVE, prsb[0][0][:, 0:1], 0.0, [prsb[0][1]])
        for b4 in range(4):
            prs, t_prs = prsb[b4 % 2]
            prn, t_prn = prsb[(b4 + 1) % 2]
            t1s, t_t1s = t1sb[b4 % 2]
            ps, tp = psum()
            for dc in range(8):
                P.mm(ps, wb[:, dc, :], hT[:, dc, b4 * 512:b4 * 512 + 512], [twb] + hT_all[b4 * 4:b4 * 4 + 4], [tp],
                     start=(dc == 0), stop=(dc == 7))
            P.copy(ACT, prs[:, 1:513], ps, [tp], [t_prs])
            P.copy(ACT, prn[:, 0:1], prs[:, 512:513], [t_prs], [t_prn])
            P.ts(DVE, t1s, prs[:, 0:512], pcols[:, MU0 + cc:MU0 + cc + 1], ALU.mult, [t_prs, tC], [t_t1s])
            dst = sh12 if cc == 12 else sh13
            sl = slice(b4 * 512, b4 * 512 + 512)
            P.stt(t1s, prs[:, 1:513], omm[:, cc:cc + 1], t1s, ALU.mult, ALU.add, [t_prs, t_t1s, tC], [t_t1s])
            if cc == 12:
                P.act(dst[0:64, sl], t1s[0:64, :], AF.Tanh, [t_t1s], [t_sh[b4]])
                P.copy(DVE, dst[64:128, sl], t1s[64:128, :], [t_t1s], [t_sh[b4]])
            else:
                P.act(dst[:, sl], t1s, AF.Sigmoid, [t_t1s], [t_sh[b4]])
    P.barrier()
    A.reset(mB0)

    if upto == 'B0':
        P.replay()
        return nc
    wrkv = [[(A.alloc([128, 8, 128], BF16, "wrkv"), Tok("wrkv")) for _ in range(3)] for _ in range(1)]
    MT_all = A.alloc([128, 16, 64], F32, "MT")
    Nn_all = A.alloc([128, 16, 64], F32, "Nn")
    RhT_all = A.alloc([128, 16, 128], F32, "RhT")
    YlT_all = A.alloc([128, 16, 128], BF16, "YlT")
    gw = A.alloc([128, S], BF16, "gw")
    bg = A.alloc([128, S], BF16, "bg")
    Hall = A.alloc([128, 17, 64], F32, "Hall")
    t_pp = [Tok("pp%d" % i) for i in range(NB)]
    t_H = Tok("H")
    pr = [[(A.alloc([128, 1 + W], F32, "pr"), Tok("pr")) for _ in range(2)] for _ in range(3)]

    def T_(shape, dt=F32, name="tmp"):
        return A.alloc(shape, dt, name), Tok(name)

    X1, t_X1 = T_([128, W], name="X1")
    X2, t_X2 = T_([128, W], name="X2")
    t1, t_t1 = X1, t_X1
    csx, t_csx = X1, t_X1
    bv, t_bv = X1, t_X1
    kk2, t_kk2 = X2, t_X2
    km, t_km = X2, t_X2
    rk, t_rk = X2, t_X2
    r_s, t_r = T_([128, W], name="r_s")
    k_s, t_k = T_([128, W], name="k_s")
    v_s, t_v = T_([128, W], name="v_s")
    sig, t_sig = T_([128, W], name="sig")
    a_s, t_a = T_([128, W], name="a_s")
    g_s, t_g = T_([128, W], name="g_s")
    cs, t_cs = T_([128, W], name="cs")
    Eex, t_Eex = T_([128, W], name="Eex")
    Eneg, t_Eneg = T_([128, W], name="Eneg")
    kk, t_kk = T_([128, W], name="kk")
    rn, t_rn = X2, t_X2
    BKh, t_BKh = T_([128, 2 * 256], BF16, name="BKh")
    ABT, t_ABT = T_([128, 4, 2, 128], BF16, name="ABT")
    AKT, t_AKT = T_([128, 4, 2, 128], BF16, name="AKT")
    u4 = lambda ap: ap.rearrange("p (u t) -> p u t", u=4)
    Lf = [T_([128, 512], BF16, name="L") for _ in range(2)]
    Pf = [T_([128, 512], BF16, name="Pm") for _ in range(2)]
    Lb = [(u4(a), t) for a, t in Lf]
    Pb = [(u4(a), t) for a, t in Pf]
    Zf_, t_Zf = T_([128, 512], F32, name="Zf")
    Zh_, t_Zh = T_([128, 512], BF16, name="Zh")
    Zf, Zh = u4(Zf_), u4(Zh_)
    v_bf, t_vbf = T_([128, W], BF16, name="v_bf")
    Ot = [T_([128, 512], F32, name="Ot") for _ in range(4)]
    v2 = lambda ap: ap.rearrange("p (c t) -> p c t", c=2)
    XS = []
    for i in range(2):
        d = {}
        d["AR"], d["t_AR"] = T_([128, 2 * 256], BF16, name="AR")
        d["BK"], d["t_BK"] = T_([128, 2 * 256], BF16, name="BK")
        d["TM"], d["t_TM"] = T_([128, 2, 4, 128], BF16, name="TM")
        d["Ein"], d["t_Ein"] = T_([128, W], name="Ein")
        d["gwb"], d["t_gwb"] = T_([128, W], BF16, name="gwb")
        d["bgb"], d["t_bgb"] = T_([128, W], BF16, name="bgb")
        d["AR4"] = d["AR"].rearrange("p (c k t) -> p c k t", c=2, k=2)
        d["BK4"] = d["BK"].rearrange("p (c k t) -> p c k t", c=2, k=2)
        XS.append(d)

    Yb, t_Yb = Ot[0]
    Yc, t_Yc = Ot[1]
    Ysq, t_Ysq = Ot[2]
    rstd, t_rstd = Ot[3]

    ring1 = [0]
    ring2 = [0]

    def psum1():
        bk = banks[ring1[0] % 4]
        ring1[0] += 1
        return bk

    zheld = [None]

    def psum2():
        bk = banks[4 + ring2[0] % 4]
        zheld[0] = 4 + ring2[0] % 4
        ring2[0] += 1
        return bk

    def psum2b():
        while 4 + ring2[0] % 4 == zheld[0]:
            ring2[0] += 1
        bk = banks[4 + ring2[0] % 4]
        ring2[0] += 1
        return bk

    def S1(p, b):
        X = XS[b % 2]
        AR, t_AR, BK, t_BK, TM, t_TM, Ein, t_Ein = X["AR"], X["t_AR"], X["BK"], X["t_BK"], X["TM"], X["t_TM"], X["Ein"], X["t_Ein"]
        AR4, BK4 = X["AR4"], X["BK4"]
        wset = wrkv[0]
        if b == 0:
            for role in range(3):
                wb, twb = wset[role]
                cc = role * 4 + p
                P.dma(POOL, (lambda wb, cc: lambda e: e.dma_start(out=wb, in_=w_in_v[:, :, cc * 128:cc * 128 + 128]))(wb, cc),
                      "wrkv_%d" % (role,), writes=[twb])
            for role in range(3):
                P.memset(DVE, pr[role][0][0][:, 0:1], 0.0, [pr[role][0][1]])
        tsl = slice(b * W, b * W + W)
        hdeps = hT_all[b * 2:b * 2 + 2]
        outs = [(r_s, t_r), (k_s, t_k), (v_s, t_v)]
        for role in range(3):
            wb, twb = wset[role]
            cc = role * 4 + p
            ps, tp = psum1()
            for dc in range(8):
                P.mm(ps[:, 0:W], wb[:, dc, :], hT[:, dc, tsl], [twb] + hdeps, [tp], start=(dc == 0), stop=(dc == 7))
            yield
            prb, tprb = pr[role][b % 2]
            prn, tprn = pr[role][(b + 1) % 2]
            P.copy(ACT, prb[:, 1:1 + W], ps[:, 0:W], [tp], [tprb])
            P.copy(ACT, prn[:, 0:1], prb[:, W:W + 1], [tprb], [tprn])
            P.ts(DVE, t1, prb[:, 0:W], pcols[:, MU0 + cc:MU0 + cc + 1], ALU.mult, [tprb, tC], [t_t1])
            o, to = outs[role]
            P.stt(o, prb[:, 1:1 + W], omm[:, cc:cc + 1], t1, ALU.mult, ALU.add, [tprb, t_t1, tC], [to])
            yield
        sdep = [t_sh[b // 2]]
        psw, tpw = psum1()
        P.mm(psw[:, 0:W], lora[0:64, p * 128:p * 128 + 128], sh12[0:64, tsl], [tC] + sdep, [tpw])
        psa, tpa = psum1()
        P.mm(psa[:, 0:W], lora[64:128, p * 128:p * 128 + 128], sh12[64:128, tsl], [tC] + sdep, [tpa])
        psg, tpg = psum1()
        P.mm(psg[:, 0:W], lora[:, 512 + p * 128:512 + p * 128 + 128], sh13[:, tsl], [tC] + sdep, [tpg])
        yield
        P.act(sig, psw[:, 0:W], AF.Exp, [tpw, tC], [t_sig], bias=nwa[:, p:p + 1], scale=-1.0)
        P.act(a_s, psa[:, 0:W], AF.Exp, [tpa, tC], [t_a], bias=nwa[:, 4 + p:4 + p + 1], scale=-1.0)
        P.copy(ACT, g_s, psg[:, 0:W], [tpg], [t_g])
        yield
        P.act(sig, sig, AF.Ln, [t_sig, tC], [t_sig], bias=onec)
        P.act(a_s, a_s, AF.Ln, [t_a, tC], [t_a], bias=onec)
        P.act(sig, sig, AF.Exp, [t_sig], [t_sig], scale=-1.0)
        P.act(a_s, a_s, AF.Exp, [t_a], [t_a], scale=-1.0)
        yield
        P.op(DVE, lambda e: e.tensor_tensor_scan(out=cs, data0=rmask, data1=sig, initial=0.0, op0=ALU.mult,
                                                 op1=ALU.add), [t_sig, tC], [t_cs])
        P.tt(DVE, csx, cs, sig, ALU.subtract, [t_cs, t_sig], [t_csx])
        yield
        P.act(Ein, cs, AF.Exp, [t_cs], [t_Ein], scale=-CEXP)
        P.act(Eex, csx, AF.Exp, [t_csx], [t_Eex], scale=-CEXP)
        P.act(Eneg, cs, AF.Exp, [t_cs], [t_Eneg], scale=CEXP)
        yield
        P.ts(DVE, kk, k_s, pcols[:, KK0 + p:KK0 + p + 1], ALU.mult, [t_k, tC], [t_kk])
        P.act(kk2, kk, AF.Square, [t_kk], [t_kk2])
        P.mm(psg[:, W:2 * W], BD1, kk2, [tC, t_kk2], [tpg])
        yield
        P.act(rn, psg[:, W:2 * W], AF.Ln, [tpg, tC], [t_rn], bias=tinyc)
        P.act(rn, rn, AF.Exp, [t_rn], [t_rn], scale=-0.5)
        P.tt(DVE, kk, kk, rn, ALU.mult, [t_kk, t_rn], [t_kk])
        yield
        P.ts(DVE, km, a_s, pcols[:, KA0 + p:KA0 + p + 1], ALU.mult, [t_a, tC], [t_km], s2=omka[:, p:p + 1], op1=ALU.add)
        P.tt(DVE, k_s, k_s, km, ALU.mult, [t_k, t_km], [t_k])
        P.tt(DVE, bv, kk, a_s, ALU.mult, [t_kk, t_a], [t_bv])
        yield
        P.stt(AR4[:, :, 0, :], v2(kk), -1.0, v2(Eex), ALU.mult, ALU.mult, [t_kk, t_Eex], [t_AR])
        P.tt(DVE, AR4[:, :, 1, :], v2(r_s), v2(Ein), ALU.mult, [t_r, t_Ein], [t_AR])
        yield
        P.tt(DVE, BK4[:, :, 0, :], v2(bv), v2(Eneg), ALU.mult, [t_bv, t_Eneg], [t_BK])
        P.tt(DVE, BK4[:, :, 1, :], v2(k_s), v2(Eneg), ALU.mult, [t_k, t_Eneg], [t_BK])
        yield
        for c in range(2):
            P.ts(DVE, BKh[:, c * 256:c * 256 + 256], BK[:, c * 256:c * 256 + 256], Ein[:, c * 128 + 127:c * 128 + 128],
                 ALU.mult, [t_BK, t_Ein], [t_BKh])
        yield
        P.stt(rk, r_s, pcols[:, RK0 + p:RK0 + p + 1], k_s, ALU.mult, ALU.mult, [t_r, t_k, tC], [t_rk])
        psb, tpb = psum1()
        P.mm(psb[:, 0:W], BD1, rk, [tC, t_rk], [tpb])
        yield
        P.tt(DVE, rk, psb[:, 0:W], v_s, ALU.mult, [tpb, t_v], [t_rk])
        P.ts(DVE, X["gwb"], g_s, pcols[:, LNW0 + p:LNW0 + p + 1], ALU.mult, [t_g, tC], [X["t_gwb"]])
        P.stt(X["bgb"], rk, pcols[:, LNB0 + p:LNB0 + p + 1], g_s, ALU.add, ALU.mult, [t_rk, t_g, tC], [X["t_bgb"]])
        yield
        P.copy(POOL, v_bf, v_s, [t_v], [t_vbf])
        for c in range(2):
            pst, tpt = psum1()
            srcs = [(AR[:, c * 256:c * 256 + 128], t_AR), (BKh[:, c * 256:c * 256 + 128], t_BKh),
                    (BKh[:, c * 256 + 128:c * 256 + 256], t_BKh), (v_bf[:, c * 128:c * 128 + 128], t_vbf)]
            pstb = pst.bitcast(BF16)
            for j, (sap, stok) in enumerate(srcs):
                P.tr(pstb[:, 128 * j:128 * j + 128], sap, identb, [stok, tC], [tpt])
            P.copy(ACT, TM[:, c, :, :], pstb[:, 0:512].rearrange("p (j t) -> p j t", j=4), [tpt], [t_TM])
            yield

    def S2(p, b):
        X = XS[b % 2]
        AR, t_AR, BK, t_BK, TM, t_TM, Ein, t_Ein = X["AR"], X["t_AR"], X["BK"], X["t_BK"], X["TM"], X["t_TM"], X["Ein"], X["t_Ein"]
        AR4 = X["AR4"]
        tsl = slice(b * W, b * W + W)
        P.copy(POOL, gw[:, tsl], X["gwb"], [X["t_gwb"]], [t_pp[b]])
        P.copy(POOL, bg[:, tsl], X["bgb"], [X["t_bgb"]], [t_pp[b]])
        units = [(c, q) for c in range(2) for q in range(2)]
        for which, dstT, tdst in ((0, ABT, t_ABT), (1, AKT, t_AKT)):
            pss = [psum2(), psum2()]
            for c in range(2):
                for q in range(2):
                    pq = slice(64 * q, 64 * q + 64)
                    ps, tp = pss[q]
                    lhs = BK[pq, c * 256 + which * 128:c * 256 + which * 128 + 128]
                    P.mm(ps[:, c * 256:c * 256 + 256], lhs, AR[pq, c * 256:c * 256 + 256], [t_BK, t_AR], [tp])
            for q in range(2):
                ps, tp = pss[q]
                P.tt(DVE, dstT[:, q::2, :, :], ps.rearrange("p (u k t) -> p u k t", u=2, k=2),
                     maskA[:, 0:2, :, :], ALU.mult, [tp, tC], [tdst])
            yield
        L0, tL0 = Lb[0]
        for q in range(2):
            pq = slice(64 * q, 64 * q + 64)
            ps, tp = psum2()
            for c in range(2):
                P.mm(ps[:, c * 128:c * 128 + 128], AR[pq, c * 256:c * 256 + 128], BK[pq, c * 256:c * 256 + 128],
                     [t_AR, t_BK], [tp])
            P.tt(DVE, L0[:, q::2, :], ps[:, 0:256].rearrange("p (u t) -> p u t", u=2), maskL[:, 0:2, :], ALU.mult,
                 [tp, tC], [tL0])
            yield
        psZ, tpZ = psum2()
        first = True
        for u, (c, q) in enumerate(units):
            P.mm(psZ[:, u * 128:u * 128 + 64], identb, TM[:, c, 0, 64 * q:64 * q + 64], [tC, t_TM], [tpZ],
                 start=first, stop=False, sgc=True)
            first = False
            P.mm(psZ[:, u * 128 + 64:u * 128 + 128], AKT[:, u, 0, :], TM[:, c, 3, 64 * q:64 * q + 64], [t_AKT, t_TM], [tpZ],
                 start=False, stop=(u == 3), sgc=True)
        yield
        P.copy(ACT, Zh_, psZ, [tpZ], [t_Zh])
        yield
        Pk, tPk = ABT, t_ABT
        for lev in range(7):
            Lk, tLk = Lb[lev % 2]
            Ln, tLn = Lb[(lev + 1) % 2]
            Pn, tPn = Pb[lev % 2]
            pk = (lambda u: ABT[:, u, 0, :]) if lev == 0 else (lambda u, Pk=Pk: Pk[:, u, :])
            if lev < 6:
                ps1, tp1 = psum2b()
                ps2, tp2 = psum2b()
                for u in range(4):
                    P.mm(ps1[:, u * 128:u * 128 + 128], Lk[:, u, :], pk(u), [tLk, tPk], [tp1])
                    P.mm(ps2[:, u * 128:u * 128 + 128], pk(u), Lk[:, u, :], [tLk, tPk], [tp2])
            for u in range(4):
                P.mm(psZ[:, u * 128:u * 128 + 128], pk(u), Zh[:, u, :], [tPk, t_Zh], [tpZ], start=False, stop=(u == 3), sgc=True)
            yield
            if lev < 6:
                P.copy(DVE, Pn, ps1.rearrange("p (u t) -> p u t", u=4), [tp1], [tPn])
                P.copy(ACT, Ln, ps2.rearrange("p (u t) -> p u t", u=4), [tp2], [tLn])
                Pk, tPk = Pn, tPn
            P.copy(ACT, Zh_, psZ, [tpZ], [t_Zh])
            yield
        Z7, tZ7 = Zh, t_Zh
        psM, tpM = psum2b()
        psR, tpR = psum2b()
        for u, (c, q) in enumerate(units):
            pq = slice(64 * q, 64 * q + 64)
            tpos = (0, 64 * q)
            P.mm(psM[pq, c * 64:c * 64 + 64], Z7[:, u, 0:64], TM[:, c, 1, 64 * q:64 * q + 64], [tZ7, t_TM], [tpM], tp=tpos)
            P.mm(psM[pq, 128 + c * 64:128 + c * 64 + 64], TM[:, c, 1, 64 * q:64 * q + 64], Z7[:, u, 64:128],
                 [tZ7, t_TM], [tpM], start=True, stop=False, tp=tpos)
            P.mm(psM[pq, 128 + c * 64:128 + c * 64 + 64], TM[:, c, 2, 64 * q:64 * q + 64], TM[:, c, 3, 64 * q:64 * q + 64],
                 [t_TM], [tpM], start=False, stop=True, tp=tpos)
            P.mm(psR[pq, c * 128:c * 128 + 128], Z7[:, u, 0:64], ABT[:, u, 1, :], [tZ7, t_ABT], [tpR], tp=tpos)
            P.mm(psR[pq, 256 + c * 128:256 + c * 128 + 128], Z7[:, u, 64:128], ABT[:, u, 1, :], [tZ7, t_ABT], [tpR],
                 start=True, stop=False, tp=tpos)
            P.mm(psR[pq, 256 + c * 128:256 + c * 128 + 128], TM[:, c, 3, 64 * q:64 * q + 64], AKT[:, u, 1, :],
                 [t_TM, t_AKT], [tpR], start=False, stop=True, tp=tpos)
        yield
        for c in range(2):
            P.stt(MT_all[:, 2 * b + c, :], identpair, Ein[:, c * 128 + 127:c * 128 + 128], psM[:, c * 64:c * 64 + 64],
                  ALU.mult, ALU.add, [tpM, t_Ein, tC], [t_pp[b]])
        P.copy(ACT, Nn_all[:, 2 * b:2 * b + 2, :], psM[:, 128:256].rearrange("p (c v) -> p c v", c=2), [tpM], [t_pp[b]])
        P.tt(DVE, RhT_all[:, 2 * b:2 * b + 2, :], psR[:, 0:256].rearrange("p (c t) -> p c t", c=2), AR4[:, :, 1, :],
             ALU.add, [tpR, t_AR], [t_pp[b]])
        P.copy(ACT, YlT_all[:, 2 * b:2 * b + 2, :], psR[:, 256:512].rearrange("p (c t) -> p c t", c=2), [tpR], [t_pp[b]])
        yield
        if b == NB - 1:
            yield from S3(p)

    def S3(p):
        P.memset(DVE, Hall[:, 0, :], 0.0, [t_H])
        for c in range(16):
            ps, tp = psum2()
            for q in range(2):
                pq = slice(64 * q, 64 * q + 64)
                P.mm(ps[pq, 0:64], MT_all[pq, c, :], Hall[pq, c, :], [t_pp[c // 2], t_H], [tp], tp=(64 * q, 64 * q))
            P.tt(DVE, Hall[:, c + 1, :], ps[:, 0:64], Nn_all[:, c, :], ALU.add, [tp, t_pp[c // 2]], [t_H])
            yield
        for b4 in range(4):
            ps, tp = psum2()
            for cj in range(4):
                c = b4 * 4 + cj
                for q in range(2):
                    pq = slice(64 * q, 64 * q + 64)
                    P.mm(ps[pq, cj * 128:cj * 128 + 128], Hall[pq, c, :], RhT_all[pq, c, :], [t_H, t_pp[c // 2]], [tp],
                         tp=(64 * q, 64 * q))
            P.tt(DVE, u4(Yb), u4(ps), YlT_all[:, b4 * 4:b4 * 4 + 4, :], ALU.add,
                 [tp, t_pp[b4 * 2], t_pp[b4 * 2 + 1]], [t_Yb])
            if p == 0 and b4 == 0:
                dump("Y0", Yb, t_Yb)
            yield
            psm, tpm = psum2()
            P.mm(psm, BD64, Yb, [tC, t_Yb], [tpm])
            P.tt(DVE, Yc, Yb, psm, ALU.subtract, [t_Yb, tpm], [t_Yc])
            P.act(Ysq, Yc, AF.Square, [t_Yc], [t_Ysq])
            yield
            psv, tpv = psum2()
            P.mm(psv, BD64, Ysq, [tC, t_Ysq], [tpv])
            P.act(rstd, psv, AF.Ln, [tpv, tC], [t_rstd], bias=epsG)
            P.act(rstd, rstd, AF.Exp, [t_rstd], [t_rstd], scale=-0.5)
            yield
            sl = slice(b4 * 512, b4 * 512 + 512)
            P.tt(DVE, Yc, Yc, rstd, ALU.mult, [t_Yc, t_rstd], [t_Yc])
            P.tt(DVE, Yc, Yc, gw[:, sl], ALU.mult, [t_Yc, t_pp[b4 * 2], t_pp[b4 * 2 + 1]], [t_Yc])
            P.tt(DVE, yT[:, p, sl], Yc, bg[:, sl], ALU.add, [t_Yc, t_pp[b4 * 2], t_pp[b4 * 2 + 1]], [t_yT[p]])
            yield

    NBLK = 4 * NB
    if upto.startswith('B') and len(upto) == 3:
        NBLK = int(upto[2])
    i1 = 0
    f1 = 0
    f2 = 0
    g1 = None
    g2 = None
    while f2 < NBLK:
        if g1 is None and i1 < NBLK and i1 - f2 <= 1:
            g1 = S1(i1 // NB, i1 % NB)
            i1 += 1
        if g2 is None and f2 < f1:
            g2 = S2(f2 // NB, f2 % NB)
        if g1 is not None:
            try:
                next(g1)
            except StopIteration:
                g1 = None
                f1 += 1
        if g2 is not None:
            try:
                next(g2)
            except StopIteration:
                g2 = None
                f2 += 1
    dump("yT0", yT[:, 0, :], t_yT[0])
    P.barrier()
    A.reset(phase_mark)
    if upto == 'B':
        P.replay()
        return nc

    A.cur += 16 * D * 4
    wo, t_wo = T_([128, 8, D], BF16, "wo")
    wo_end = A.mark()
    A.reset(phase_mark)
    for half in range(2):
        P.dma(POOL, (lambda half: lambda e: e.dma_start(out=wo[:, :, half * 512:half * 512 + 512],
                                                        in_=w_out_v[:, :, half * 512:half * 512 + 512]))(half),
              "wo", writes=[t_wo])
    watt = [[(A.alloc([128, 8, 128], BF16, "watt"), Tok("watt")) for _ in range(3)] for _ in range(2)]
    QT, t_QT = T_([128, S], BF16, "QT")
    KT, t_KT = T_([128, S], BF16, "KT")
    Vpad = [T_([128, 16, 2, 128], BF16, "Vpad") for _ in range(2)]
    acc_o, t_acco = T_([128, S], F32, "acc_o")
    acc_d, t_accd = T_([128, S], F32, "acc_d")
    peT = [T_([128, 512], BF16, "peT") for _ in range(4)]
    otmp, t_otmp = T_([128, 512], F32, "otmp")
    osq, t_osq = T_([128, 512], F32, "osq")
    orst, t_orst = osq, t_osq
    VT, t_VT = T_([128, S], BF16, "VT")
    for vb, tvb in Vpad:
        P.memset(POOL, vb, 0.0, [tvb])

    def tokset(br, blk):
        if br == 0:
            return 128 * blk, 1
        if br == 1:
            r2, n2 = blk // 4, blk % 4
            return 512 * n2 + r2, 4
        return blk, 16

    def tsl_(br, blk):
        st, sp = tokset(br, blk)
        return slice(st, st + sp * 127 + 1, sp)

    def has_prev(br, blk):
        if br == 0:
            return blk >= 1
        if br == 1:
            return blk % 4 >= 1
        return False

    ringC = [0]

    def psumC():
        bk = banks[4 + ringC[0] % 4]
        ringC[0] += 1
        return bk

    vcount = 0
    pecount = 0
    gcount = 0
    for p in range(4):
        wset = watt[p % 2]
        for role in range(3):
            wb, twb = wset[role]
            cc = 14 + role * 4 + p
            P.dma(POOL, (lambda wb, cc: lambda e: e.dma_start(out=wb, in_=w_in_v[:, :, cc * 128:cc * 128 + 128]))(wb, cc),
                  "watt%d_%d" % (p % 2, role), writes=[twb])
        for role, (dst, tdst) in enumerate(((QT, t_QT), (KT, t_KT))):
            wb, twb = wset[role]
            for b4 in range(4):
                ps, tp = psumC()
                for dc in range(8):
                    P.mm(ps, wb[:, dc, :], hT[:, dc, b4 * 512:b4 * 512 + 512], [twb] + hT_all[b4 * 4:b4 * 4 + 4], [tp],
                         start=(dc == 0), stop=(dc == 7))
                P.act(dst[:, b4 * 512:b4 * 512 + 512], ps, AF.Copy, [tp], [tdst], scale=(0.125 if role == 0 else 1.0))
        wv, twv = wset[2]
        vbufs = {}

        for b4 in range(4):
            ps, tp = psumC()
            for dc in range(8):
                P.mm(ps, wv[:, dc, :], hT[:, dc, b4 * 512:b4 * 512 + 512], [twv] + hT_all[b4 * 4:b4 * 4 + 4], [tp],
                     start=(dc == 0), stop=(dc == 7))
            P.copy(DVE, VT[:, b4 * 512:b4 * 512 + 512], ps, [tp], [t_VT])

        def vproj(br):
            nonlocal vcount
            vb, tvb = Vpad[vcount % 2]
            vcount += 1
            vbufs[br] = (vb, tvb)
            for g4 in range(4):
                ps, tp = psumC()
                psb = ps.bitcast(BF16)
                for j in range(4):
                    blk = g4 * 4 + j
                    P.tr(psb[:, j * 128:j * 128 + 128], VT[:, tsl_(br, blk)], identb, [t_VT, tC], [tp])
                psv4 = psb[:, 0:512].rearrange("p (j q e) -> p j q e", j=4, q=2)
                for q in range(2):
                    P.copy(ACT if q == 0 else DVE, vb[:, g4 * 4:g4 * 4 + 4, q, 64 * q:64 * q + 64], psv4[:, :, q, :], [tp], [tvb])

        def Xs(br, g4, jp):
            nonlocal pecount
            js = (2 * jp, 2 * jp + 1)
            blks = [g4 * 4 + j for j in js]
            hps = [has_prev(br, blk) for blk in blks]
            mb = MB_TT if (hps[0] and hps[1]) else (MB_FT if hps[1] else MB_FF)
            pes = []
            pss = [psumC(), psumC()]
            for ji in range(2):
                blk = blks[ji]
                qs = tsl_(br, blk)
                if hps[ji]:
                    for q in range(2):
                        pq = slice(64 * q, 64 * q + 64)
                        psS, tpS = pss[q]
                        P.mm(psS[:, ji * 256:ji * 256 + 128], KT[pq, tsl_(br, blk - 1)], QT[pq, qs], [t_KT, t_QT], [tpS])
                for q in range(2):
                    pq = slice(64 * q, 64 * q + 64)
                    psS, tpS = pss[q]
                    P.mm(psS[:, ji * 256 + 128:ji * 256 + 256], KT[pq, qs], QT[pq, qs], [t_KT, t_QT], [tpS])
            for q in range(2):
                psS, tpS = pss[q]
                pe, tpe = peT[pecount % 4]
                pecount += 1
                if hps[0] and hps[1]:
                    v = lambda a: a
                elif hps[1]:
                    v = lambda a: a[:, 128:512]
                else:
                    v = lambda a: a.rearrange("p (j k t) -> p j k t", j=2, k=2)[:, :, 1, :]
                P.act(v(pe), v(psS), AF.Exp, [tpS], [tpe])
                P.tt(DVE, v(pe), v(pe), v(mb), ALU.mult, [tpe, tC], [tpe])
                pes.append((pe, tpe))
            return (js, blks, hps, pes)

        gstate = {}

        def Ys(br, g4, jp, xs):
            nonlocal gcount
            js, blks, hps, pes = xs
            vb, tvb = vbufs[br]
            if jp == 0:
                pb = (gcount % 2) * 2
                gcount += 1
                gstate[(br, g4)] = (banks[pb], banks[pb + 1])
            (psO, tpO), (psD, tpD) = gstate[(br, g4)]
            for ji in range(2):
                j = js[ji]
                blk = blks[ji]
                mms = []
                for q in range(2):
                    pe, tpe = pes[q]
                    if hps[ji]:
                        mms.append((q, blk - 1, pe[:, ji * 256:ji * 256 + 128], tpe))
                    mms.append((q, blk, pe[:, ji * 256 + 128:ji * 256 + 256], tpe))
                for i, (q, kb, rhs, tpe) in enumerate(mms):
                    P.mm(psO[:, j * 128:j * 128 + 128], vb[:, kb, q, :], rhs, [tvb, tpe], [tpO],
                         start=(i == 0), stop=(i == len(mms) - 1))
                for i, (q, kb, rhs, tpe) in enumerate(mms):
                    P.mm(psD[:, j * 128:j * 128 + 128], onespad[:, q, :], rhs, [tC, tpe], [tpD],
                         start=(i == 0), stop=(i == len(mms) - 1))
            if jp == 1:
                if br == 0:
                    oa = acc_o[:, g4 * 512:g4 * 512 + 512]
                    da = acc_d[:, g4 * 512:g4 * 512 + 512]
                    P.copy(ACT, oa, psO, [tpO], [t_acco])
                    P.copy(DVE, da, psD, [tpD], [t_accd])
                else:
                    if br == 1:
                        view = lambda a: a.rearrange("p (n i r) -> p r n i", n=4, r=4)[:, g4, :, :]
                    else:
                        view = lambda a: a.rearrange("p (i r) -> p r i", r=16)[:, g4 * 4:g4 * 4 + 4, :]
                    P.tt(DVE, view(acc_o), view(acc_o), psO.rearrange("p (j i) -> p j i", j=4), ALU.add, [tpO, t_acco], [t_acco])
                    P.tt(DVE, view(acc_d), view(acc_d), psD.rearrange("p (j i) -> p j i", j=4), ALU.add, [tpD, t_accd], [t_accd])

        U = [(br, g4, jp) for br in range(3) for g4 in range(4) for jp in range(2)]
        vproj(0)
        xs_next = Xs(*U[0])
        for k in range(len(U)):
            xs_cur = xs_next
            if k + 1 < len(U):
                if U[k + 1][0] != U[k][0]:
                    vproj(U[k + 1][0])
                xs_next = Xs(*U[k + 1])
            Ys(*U[k], xs_cur)
        for b4 in range(4):
            sl = slice(b4 * 512, b4 * 512 + 512)
            P.act(otmp, acc_d[:, sl], AF.Ln, [t_accd], [t_otmp])
            P.act(otmp, otmp, AF.Exp, [t_otmp], [t_otmp], scale=-1.0)
            P.tt(DVE, otmp, otmp, acc_o[:, sl], ALU.mult, [t_otmp, t_acco], [t_otmp])
            P.act(osq, otmp, AF.Square, [t_otmp], [t_osq])
            ps, tp = psumC()
            P.mm(ps, BD64, osq, [tC, t_osq], [tp])
            P.act(orst, ps, AF.Ln, [tp, tC], [t_orst], bias=epsN)
            P.act(orst, orst, AF.Exp, [t_orst], [t_orst], scale=-0.5)
            P.stt(yT[:, 4 + p, sl], otmp, pcols[:, OG0 + p:OG0 + p + 1], orst, ALU.mult, ALU.mult, [t_otmp, t_orst, tC],
                  [t_yT[4 + p]])
    dump("yT4", yT[:, 4, :], t_yT[4])
    assert A.mark() <= phase_mark + 16 * D * 4, "phase C buffers overlap the prefetched w_out"
    P.barrier()
    A.reset(phase_mark)
    if upto == 'C':
        P.replay()
        return nc

    x1, t_x1d = T_([128, 16, D], F32, "x1")
    t_x1 = [Tok("x1_%d" % i) for i in range(16)]
    x1_end = A.mark()
    A.reset(wo_end)
    xr = [T_([128, D], F32, "xr") for _ in range(2)]
    for tt_ in range(16):
        xa, txa = xr[tt_ % 2]
        P.dma(SP, (lambda xa, i: lambda e: e.dma_start(out=xa, in_=x_v[:, i, :]))(xa, tt_), "xr%d" % (tt_ % 2), writes=[txa])
        for ch in range(2):
            ps, tp = psum()
            for cc in range(8):
                P.mm(ps, yT[:, cc, tt_ * 128:tt_ * 128 + 128], wo[:, cc, ch * 512:ch * 512 + 512], [t_yT[cc], t_wo], [tp],
                     start=(cc == 0), stop=(cc == 7))
            P.tt(DVE, x1[:, tt_, ch * 512:ch * 512 + 512], ps, xa[:, ch * 512:ch * 512 + 512], ALU.add, [tp, txa], [t_x1[tt_]])
    dump("x1", x1[:, 0, :], t_x1[0])
    P.barrier()
    A.reset(persist_mark)
    h2Ts = [T_([128, 8, 512], BF16, "h2T") for _ in range(2)]
    aT, t_aTd = T_([128, NFF, 512], BF16, "aT")
    t_aT = [Tok("aT%d" % i) for i in range(NFF)]
    scrD = [(A.alloc([128, D], F32, "hn"), Tok("hn"), A.alloc([128, D], BF16, "sq"), Tok("sq")) for _ in range(1)]
    wgu = [[T_([128, 8, 128], BF16, "wgu") for _ in range(2)] for _ in range(3)]
    sg = [T_([128, 512], BF16, "sg") for _ in range(2)]
    xo = [T_([128, D], F32, "xo") for _ in range(1)]
    assert A.mark() <= phase_mark, "overlay exceeds dead hT/yT region"
    A.reset(x1_end)
    wd, t_wdd = T_([128, NFF, D], BF16, "wd")
    t_wd = [Tok("wd%d" % j) for j in range(NFF)]
    ncol = 16

    def norm_q(qt):
        nonlocal ncol
        h2T, t_h2 = h2Ts[qt % 2]
        for tl in range(4):
            tt_ = qt * 4 + tl
            norm_p1(x1[:, tt_, :], t_x1[tt_], g2bc, scrD[0], ncol)
            ncol += 1
            norm_p2(scrD[0], h2T, tl * 128, t_h2, psum)

    gucount = 0
    norm_q(0)
    for qt in range(4):
        h2T, t_h2 = h2Ts[qt % 2]
        for j in range(NFF):
            wg_, twg = wgu[gucount % 3][0]
            wu_, twu = wgu[gucount % 3][1]
            key = gucount % 3
            gucount += 1
            P.dma(POOL, (lambda wg_, j: lambda e: e.dma_start(out=wg_, in_=w_gate_v[:, :, j * 128:j * 128 + 128]))(wg_, j),
                  "wg%d" % key, writes=[twg])
            P.dma(POOL, (lambda wu_, j: lambda e: e.dma_start(out=wu_, in_=w_up_v[:, :, j * 128:j * 128 + 128]))(wu_, j),
                  "wu%d" % key, writes=[twu])
            if qt == 0:
                P.dma(POOL, (lambda j: lambda e: e.dma_start(out=wd[:, j, :], in_=w_down_v[:, j, :]))(j), "wd%d" % j,
                      writes=[t_wd[j]])
            psg, tpg = psum()
            psu, tpu = psum()
            for dc in range(8):
                P.mm(psg, wg_[:, dc, :], h2T[:, dc, :], [twg, t_h2], [tpg], start=(dc == 0), stop=(dc == 7))
            for dc in range(8):
                P.mm(psu, wu_[:, dc, :], h2T[:, dc, :], [twu, t_h2], [tpu], start=(dc == 0), stop=(dc == 7))
            sgb, tsg = sg[j % 2]
            P.act(sgb, psg, AF.Silu, [tpg], [tsg])
            P.tt(DVE, aT[:, j, :], sgb, psu, ALU.mult, [tsg, tpu], [t_aT[j]])
        if qt + 1 < 4:
            norm_q(qt + 1)
        for tl in range(4):
            tt_ = qt * 4 + tl
            xob, txo = xo[0]
            for ch in range(2):
                ps, tp = psum()
                for j in range(NFF):
                    P.mm(ps, aT[:, j, tl * 128:tl * 128 + 128], wd[:, j, ch * 512:ch * 512 + 512], [t_aT[j], t_wd[j]], [tp],
                         start=(j == 0), stop=(j == NFF - 1))
                P.tt(DVE, xob[:, ch * 512:ch * 512 + 512], ps, x1[:, tt_, ch * 512:ch * 512 + 512], ALU.add,
                     [tp, t_x1[tt_]], [txo])
            hn_, t_hn_, sc, tsc = scrD[0]
            tS_ = tSs[ncol]
            P.act(sc, xob, AF.Square, [txo], [tsc, tS_], accum=ssq[:, ncol:ncol + 1])
            P.act(ssq[:, ncol:ncol + 1], ssq[:, ncol:ncol + 1], AF.Ln, [tS_, tC], [tS_], bias=epsN, scale=1.0 / D)
            P.act(ssq[:, ncol:ncol + 1], ssq[:, ncol:ncol + 1], AF.Exp, [tS_], [tS_], scale=-0.5)
            P.stt(xob, xob, ssq[:, ncol:ncol + 1], gfbc, ALU.mult, ALU.mult, [txo, tS_, tC], [txo])
            ncol += 1
            P.dma(SP, (lambda xob, i: lambda e: e.dma_start(out=out_v[:, i, :], in_=xob))(xob, tt_), "xo0",
                  reads=[txo], is_out=True)
    P.replay()
    return nc


_NC = {}


def _host_layout(inp):
    f = lambda a: np.ascontiguousarray(np.asarray(a, dtype=np.float32))
    col = lambda v: f(v).reshape(-1, 128).T
    pcols = np.concatenate([
        col(inp["mu_shift"][0]), col(inp["decay_w0"][0]), col(inp["iclr_a0"][0]), col(inp["k_k"][0]),
        col(inp["k_a"][0]), col(inp["r_k"][0].reshape(-1)), col(inp["ln_x_w"][0]), col(inp["ln_x_b"][0]),
        col(inp["attn_out_g"][0])], axis=1)
    assert pcols.shape == (128, NPC)
    gbc = np.stack([np.broadcast_to(f(inp["mix_norm_g"][0])[None, :], (128, D)),
                    np.broadcast_to(f(inp["ffn_norm_g"][0])[None, :], (128, D)),
                    np.broadcast_to(f(inp["final_norm_g"])[None, :], (128, D))], axis=0)
    lora = np.concatenate([np.concatenate([f(inp["decay_w2"][0]), f(inp["iclr_a2"][0])], axis=0),
                           f(inp["gate_g2"][0])], axis=1)
    shared = {
        "w_in": f(inp["w_in"][0]), "w_out": f(inp["w_out"][0]), "w_gate": f(inp["w_gate"][0]),
        "w_up": f(inp["w_up"][0]), "w_down": f(inp["w_down"][0]),
        "pcols": f(pcols), "gbc": f(gbc), "lora": f(lora),
    }
    return shared


def kernel(**inputs):
    x = np.asarray(inputs["x"], dtype=np.float32)
    shared = _host_layout(inputs)
    if "nc" not in _NC:
        _NC["nc"] = build()
    nc = _NC["nc"]
    in_maps = []
    for b in range(8):
        m = dict(shared)
        m["x"] = np.ascontiguousarray(x[b])
        in_maps.append(m)
    res = run_bass_kernel_spmd(nc, in_maps, core_ids=list(range(8)))
    return np.stack([r["out"] for r in res.results], axis=0).astype(np.float32)
```

```python
import contextlib
import numpy as np
import concourse.bass as bass
import concourse.mybir as mybir
from concourse.bass_utils import run_bass_kernel_spmd

F32 = mybir.dt.float32
BF16 = mybir.dt.bfloat16
AF = mybir.ActivationFunctionType
ALU = mybir.AluOpType

PE, ACT, DVE, POOL, SP = "tensor", "scalar", "vector", "gpsimd", "sync"
ENGINES = (PE, ACT, DVE, POOL, SP)

S = 2048
D = 1024
DFF = 2816
NFF = 22
INC = 3328
CEXP = float(np.exp(-0.5))
NEG = -30000.0
MU0, W00, A00, KK0, KA0, RK0, LNW0, LNB0, OG0, NPC = 0, 14, 18, 22, 26, 30, 34, 38, 42, 46

DEBUG = {}


class Tok:
    __slots__ = ("w", "r", "name")

    def __init__(self, name=""):
        self.w = None
        self.r = {}
        self.name = name


class Prog:
    def __init__(self, nc):
        self.nc = nc
        self.streams = {e: [] for e in ENGINES}
        self.seq = {e: 0 for e in ENGINES}
        self.dma_cnt = {}
        self.out_dma = []
        self.needed = {e: set() for e in ENGINES}
        self.pending = {e: None for e in ENGINES}
        self.nuniq = 0

    def _gather(self, eng, reads, writes):
        deps = set()
        for t in reads:
            if t.w is not None:
                deps.add(t.w)
        for t in writes:
            if t.w is not None:
                deps.add(t.w)
            deps.update(t.r.values())
        if self.pending[eng] is not None:
            deps.update(self.pending[eng])
            self.pending[eng] = None
        out = []
        for d in deps:
            if d[0] == 'e' and d[1] == PE and eng == PE:
                continue
            out.append(d)
            if d[0] == 'e':
                self.needed[d[1]].add(d[2])
        return out

    def _mark(self, me, reads, writes):
        key = (me[0], me[1])
        for t in reads:
            t.r[key] = me
        for t in writes:
            t.w = me
            t.r = {}

    def op(self, eng, fn, reads=(), writes=()):
        deps = self._gather(eng, reads, writes)
        self.seq[eng] += 1
        me = ('e', eng, self.seq[eng])
        self._mark(me, reads, writes)
        self.streams[eng].append((deps, fn, me))
        return me

    def dma(self, eng, fn, semkey, reads=(), writes=(), is_out=False):
        deps = self._gather(eng, reads, writes)
        self.dma_cnt[semkey] = self.dma_cnt.get(semkey, 0) + 1
        me = ('d', semkey, self.dma_cnt[semkey])
        self._mark(me, reads, writes)
        self.streams[eng].append((deps, fn, me))
        if is_out:
            self.out_dma.append(me)
        return me

    def barrier(self):
        deps = set()
        for e in ENGINES:
            if e != SP and self.seq[e] > 0:
                deps.add(('e', e, self.seq[e]))
        for k, c in self.dma_cnt.items():
            deps.add(('d', k, c))
        for e in ENGINES:
            cur = self.pending[e] or set()
            self.pending[e] = set(cur) | deps

    def replay(self):
        nc = self.nc
        ranks = {}
        for e in ENGINES:
            srt = sorted(self.needed[e])
            ranks[e] = {s: i + 1 for i, s in enumerate(srt)}
        with contextlib.ExitStack() as st:
            esem = {e: st.enter_context(nc.semaphore("s_" + e)) for e in ENGINES}
            dsem = {k: st.enter_context(nc.semaphore("d_%s" % (k,))) for k in self.dma_cnt}
            block = st.enter_context(nc.Block())

            def resolve(d):
                if d[0] == 'e':
                    return esem[d[1]], ranks[d[1]][d[2]]
                return dsem[d[1]], 16 * d[2]

            def run(ename, handle):
                waited = {}
                for deps, fn, me in self.streams[ename]:
                    for d in deps:
                        sem, val = resolve(d)
                        k = (d[0], d[1])
                        if waited.get(k, 0) < val:
                            handle.wait_ge(sem, val)
                            waited[k] = val
                    inst = fn(handle)
                    if me[0] == 'e':
                        if me[2] in ranks[ename]:
                            inst.then_inc(esem[ename], 1)
                    else:
                        inst.then_inc(dsem[me[1]], 16)
                if ename == SP:
                    for d in self.out_dma:
                        sem, val = resolve(d)
                        handle.wait_ge(sem, val)

            @block.tensor
            def _(h):
                run(PE, h)

            @block.scalar
            def _(h):
                run(ACT, h)

            @block.vector
            def _(h):
                run(DVE, h)

            @block.gpsimd
            def _(h):
                run(POOL, h)

            @block.sync
            def _(h):
                run(SP, h)

    def mm(self, out, lhsT, rhs, R, W, start=True, stop=True, tp=None, sgc=False):
        if sgc:
            return self.op(PE, lambda e: e.matmul(out, lhsT=lhsT, rhs=rhs, start=start, stop=stop, skip_group_check=True), R, W)
        if tp is None:
            return self.op(PE, lambda e: e.matmul(out, lhsT=lhsT, rhs=rhs, start=start, stop=stop), R, W)
        return self.op(PE, lambda e: e.matmul(out, lhsT=lhsT, rhs=rhs, start=start, stop=stop, tile_position=tp), R, W)

    def tr(self, out, in_, ident, R, W):
        return self.op(PE, lambda e: e.transpose(out=out, in_=in_, identity=ident), R, W)

    def act(self, out, in_, func, R, W, bias=None, scale=None, accum=None):
        kw = {}
        if bias is not None:
            kw["bias"] = bias
        if scale is not None:
            kw["scale"] = scale
        if accum is not None:
            kw["accum_out"] = accum
        return self.op(ACT, lambda e: e.activation(out=out, in_=in_, func=func, **kw), R, W)

    def tt(self, eng, out, a, b, op, R, W):
        return self.op(eng, lambda e: e.tensor_tensor(out=out, in0=a, in1=b, op=op), R, W)

    def ts(self, eng, out, a, s1, op0, R, W, s2=None, op1=None):
        if op1 is None:
            return self.op(eng, lambda e: e.tensor_scalar(out=out, in0=a, scalar1=s1, scalar2=None, op0=op0), R, W)
        return self.op(eng, lambda e: e.tensor_scalar(out=out, in0=a, scalar1=s1, scalar2=s2, op0=op0, op1=op1), R, W)

    def stt(self, out, a, s, b, op0, op1, R, W):
        return self.op(DVE, lambda e: e.scalar_tensor_tensor(out=out, in0=a, scalar=s, in1=b, op0=op0, op1=op1), R, W)

    def copy(self, eng, out, in_, R, W):
        if eng == ACT:
            return self.op(ACT, lambda e: e.copy(out=out, in_=in_), R, W)
        return self.op(eng, lambda e: e.tensor_copy(out=out, in_=in_), R, W)

    def memset(self, eng, ap, val, W):
        return self.op(eng, lambda e: e.memset(ap, val), (), W)

    def recip(self, out, in_, R, W):
        return self.op(DVE, lambda e: e.reciprocal(out=out, in_=in_), R, W)


class Arena:
    def __init__(self, nc, base=16640, top=229000):
        self.nc, self.base, self.top, self.cur, self.n = nc, base, top, base, 0

    def alloc(self, shape, dtype, name="t"):
        nbytes = int(np.prod(shape[1:])) * (4 if dtype == F32 else 2)
        nbytes = (nbytes + 63) // 64 * 64
        if self.cur + nbytes > self.top:
            raise RuntimeError("arena overflow %s %d" % (name, self.cur + nbytes - self.top))
        self.n += 1
        t = self.nc.alloc_sbuf_tensor_at("%s_%d" % (name, self.n), list(shape), dtype, offset=self.cur)
        self.cur += nbytes
        return t.ap()

    def mark(self):
        return self.cur

    def reset(self, m):
        self.cur = m


def build(dbg=None, upto='Z'):
    nc = bass.Bass("TRN2", target_bir_lowering=False)
    P = Prog(nc)
    A = Arena(nc)

    def dram(name, shape, kind="ExternalInput"):
        return nc.dram_tensor(name, list(shape), F32, kind=kind).ap()

    x_d = dram("x", [S, D])
    w_in_d = dram("w_in", [D, INC])
    w_out_d = dram("w_out", [D, D])
    w_gate_d = dram("w_gate", [D, DFF])
    w_up_d = dram("w_up", [D, DFF])
    w_down_d = dram("w_down", [DFF, D])
    pcols_d = dram("pcols", [128, NPC])
    gbc_d = dram("gbc", [3, 128, D])
    lora_d = dram("lora", [128, 1024])
    out_d = dram("out", [S, D], kind="ExternalOutput")
    dbg_d = {}
    if dbg:
        for k, shp in dbg.items():
            dt_ = BF16 if k in ("hT", "yT0", "yT4") else F32
            dbg_d[k] = nc.dram_tensor("dbg_" + k, list(shp), dt_, kind="ExternalOutput").ap()

    x_v = x_d.rearrange("(t p) d -> p t d", p=128)
    out_v = out_d.rearrange("(t p) d -> p t d", p=128)
    w_in_v = w_in_d.rearrange("(c p) n -> p c n", p=128)
    w_out_v = w_out_d.rearrange("(c p) n -> p c n", p=128)
    w_gate_v = w_gate_d.rearrange("(c p) n -> p c n", p=128)
    w_up_v = w_up_d.rearrange("(c p) n -> p c n", p=128)
    w_down_v = w_down_d.rearrange("(c p) n -> p c n", p=128)

    banks = [(nc.alloc_psum_tensor("ps%d" % i, [128, 512], F32).ap(), Tok("ps%d" % i)) for i in range(8)]
    bank_i = [0]

    def psum():
        b = banks[bank_i[0] % 8]
        bank_i[0] += 1
        return b

    dbg_dumps = []

    def dump(key, ap, tok):
        if dbg and key in dbg_d:
            P.dma(SP, lambda e: e.dma_start(out=dbg_d[key], in_=ap), "dbg_" + key, reads=[tok], is_out=True)

    tC = Tok("consts")
    pcols = A.alloc([128, NPC], F32, "pcols")
    omm = A.alloc([128, 14], F32, "omm")
    omka = A.alloc([128, 4], F32, "omka")
    g1bc = A.alloc([128, D], F32, "g1bc")
    g2bc = A.alloc([128, D], F32, "g2bc")
    gfbc = A.alloc([128, D], F32, "gfbc")
    lora = A.alloc([128, 1024], F32, "lora")
    ones = A.alloc([128, 128], F32, "ones")
    zeros = A.alloc([128, 128], F32, "zeros")
    ident = A.alloc([128, 128], F32, "ident")
    identb = A.alloc([128, 128], BF16, "identb")
    identpair = A.alloc([128, 64], F32, "identpair")
    BD1 = A.alloc([128, 128], F32, "bd1")
    BD64 = A.alloc([128, 128], F32, "bd64")
    maskA = A.alloc([128, 2, 2, 128], F32, "maskA")
    maskL = A.alloc([128, 2, 128], F32, "maskL")
    rmask = A.alloc([128, 256], F32, "rmask")
    MB_TT = A.alloc([128, 512], BF16, "MB_TT")
    MB_FT = A.alloc([128, 512], BF16, "MB_FT")
    MB_FF = A.alloc([128, 512], BF16, "MB_FF")
    mtmp = A.alloc([128, 256], F32, "mtmp")
    onespad = A.alloc([128, 2, 128], BF16, "onespad")
    nwa = A.alloc([128, 8], F32, "nwa")
    onec = A.alloc([128, 1], F32, "onec")
    tinyc = A.alloc([128, 1], F32, "tinyc")
    epsN = A.alloc([128, 1], F32, "epsN")
    epsG = A.alloc([128, 1], F32, "epsG")
    ssq = A.alloc([128, 64], F32, "ssq")
    tS = Tok("ssq")

    P.dma(SP, lambda e: e.dma_start(out=pcols, in_=pcols_d), "consts", writes=[tC])
    P.dma(SP, lambda e: e.dma_start(out=g1bc, in_=gbc_d[0]), "consts", writes=[tC])
    P.dma(SP, lambda e: e.dma_start(out=g2bc, in_=gbc_d[1]), "consts", writes=[tC])
    P.dma(SP, lambda e: e.dma_start(out=gfbc, in_=gbc_d[2]), "consts", writes=[tC])
    P.dma(SP, lambda e: e.dma_start(out=lora, in_=lora_d), "consts", writes=[tC])
    P.memset(POOL, ones, 1.0, [tC])
    P.memset(POOL, zeros, 0.0, [tC])
    P.memset(POOL, epsN, 1e-6, [tC])
    P.memset(POOL, epsG, 64e-5, [tC])
    P.memset(POOL, BD1, 0.0, [tC])
    P.memset(POOL, BD64, 0.0, [tC])
    P.memset(POOL, onespad, 0.0, [tC])
    for q in range(2):
        P.memset(POOL, BD1[64 * q:64 * q + 64, 64 * q:64 * q + 64], 1.0, [tC])
        P.memset(POOL, BD64[64 * q:64 * q + 64, 64 * q:64 * q + 64], 1.0 / 64, [tC])
        P.memset(POOL, onespad[:, q, 64 * q:64 * q + 64], 1.0, [tC])
    P.memset(POOL, rmask, 1.0, [tC])
    P.memset(POOL, rmask[:, 0:1], 0.0, [tC])
    P.memset(POOL, rmask[:, 128:129], 0.0, [tC])

    def asel(out, in_, cmp, fill, base, cm, step):
        P.op(POOL, lambda e: e.affine_select(out=out, in_=in_, pattern=[[step, 128]], compare_op=cmp, fill=fill,
                                             base=base, channel_multiplier=cm), [tC], [tC])

    asel(ident, ones, ALU.is_equal, 0.0, 0, -1, 1)
    for u in range(2):
        asel(maskA[:, u, 0, :], ones, ALU.is_ge, 0.0, -1, -1, 1)
        asel(maskA[:, u, 1, :], ones, ALU.is_ge, 0.0, 0, -1, 1)
        asel(maskL[:, u, :], ones, ALU.is_ge, 0.0, -1, 1, -1)
    asel(mtmp[:, 0:128], ones, ALU.is_ge, 0.0, 0, 1, -1)
    asel(mtmp[:, 128:256], ones, ALU.is_ge, 0.0, 0, -1, 1)
    for h2 in range(2):
        P.copy(POOL, MB_TT[:, 256 * h2:256 * h2 + 256], mtmp, [tC], [tC])
        P.memset(POOL, MB_FF[:, 256 * h2:256 * h2 + 128], 0.0, [tC])
        P.copy(POOL, MB_FF[:, 256 * h2 + 128:256 * h2 + 256], mtmp[:, 128:256], [tC], [tC])
    P.memset(POOL, MB_FT[:, 0:128], 0.0, [tC])
    P.copy(POOL, MB_FT[:, 128:256], mtmp[:, 128:256], [tC], [tC])
    P.copy(POOL, MB_FT[:, 256:512], mtmp, [tC], [tC])
    P.copy(POOL, identb, ident, [tC], [tC])
    P.tt(POOL, identpair, ident[:, 0:64], ident[:, 64:128], ALU.add, [tC], [tC])
    P.ts(DVE, omm, pcols[:, MU0:MU0 + 14], -1.0, ALU.mult, [tC], [tC], s2=1.0, op1=ALU.add)
    P.ts(DVE, omka, pcols[:, KA0:KA0 + 4], -1.0, ALU.mult, [tC], [tC], s2=1.0, op1=ALU.add)
    P.ts(DVE, nwa, pcols[:, W00:W00 + 8], -1.0, ALU.mult, [tC], [tC])
    P.memset(DVE, onec, 1.0, [tC])
    P.memset(DVE, tinyc, 1e-24, [tC])

    persist_mark = A.mark()

    hT = A.alloc([128, 8, S], BF16, "hT")
    yT = A.alloc([128, 8, S], BF16, "yT")
    t_hT = [Tok("hT%d" % i) for i in range(16)]
    t_yT = [Tok("yT%d" % i) for i in range(8)]
    phase_mark = A.mark()

    tSs = [Tok("ssq%d" % i) for i in range(64)]

    def norm_p1(src_ap, src_tok, gbc, scr, ncol):
        hn, t_hn, sq, t_sq = scr
        tS_ = tSs[ncol]
        P.act(sq, src_ap, AF.Square, [src_tok], [t_sq, tS_], accum=ssq[:, ncol:ncol + 1])
        P.act(ssq[:, ncol:ncol + 1], ssq[:, ncol:ncol + 1], AF.Ln, [tS_, tC], [tS_], bias=epsN, scale=1.0 / D)
        P.act(ssq[:, ncol:ncol + 1], ssq[:, ncol:ncol + 1], AF.Exp, [tS_], [tS_], scale=-0.5)
        P.stt(hn, src_ap, ssq[:, ncol:ncol + 1], gbc, ALU.mult, ALU.mult, [src_tok, tS_, tC], [t_hn])

    def norm_p2(scr, dstT, dst_col0, dst_tok, psfn):
        hn, t_hn, sq, t_sq = scr
        for half in range(2):
            ps, tp = psfn()
            for j in range(4):
                dc = half * 4 + j
                P.tr(ps[:, 128 * j:128 * j + 128], hn[:, 128 * dc:128 * dc + 128], ident, [t_hn, tC], [tp])
            P.copy(ACT if half == 0 else DVE, dstT[:, half * 4:half * 4 + 4, dst_col0:dst_col0 + 128],
                   ps.rearrange("p (j t) -> p j t", j=4), [tp], [dst_tok])

    NXT = 4
    NSC = 3
    xt = [(A.alloc([128, D], F32, "xt"), Tok("xt")) for _ in range(NXT)]
    scrA = [(A.alloc([128, D], F32, "hn"), Tok("hn"), A.alloc([128, D], BF16, "sq"), Tok("sq")) for _ in range(NSC)]
    def loadx(tt_):
        xa, txa = xt[tt_ % NXT]
        P.dma(SP, (lambda xa, i: lambda e: e.dma_start(out=xa, in_=x_v[:, i, :]))(xa, tt_), "xt%d" % (tt_ % NXT),
              writes=[txa])
        return xa, txa

    xq = {}
    for tt_ in range(min(NXT, 16)):
        xq[tt_] = loadx(tt_)
    for tt_ in range(2):
        norm_p1(xq[tt_][0], xq[tt_][1], g1bc, scrA[tt_ % NSC], tt_)
    for tt_ in range(16):
        if tt_ + 2 < 16:
            if tt_ + 2 not in xq:
                xq[tt_ + 2] = loadx(tt_ + 2)
            xa, txa = xq[tt_ + 2]
            norm_p1(xa, txa, g1bc, scrA[(tt_ + 2) % NSC], tt_ + 2)
        norm_p2(scrA[tt_ % NSC], hT, tt_ * 128, t_hT[tt_], psum)
    dump("hT", hT[:, 0, :], t_hT[15])
    P.barrier()
    A.reset(phase_mark)
    if upto == 'A':
        P.replay()
        return nc

    hT_all = t_hT
    W = 256
    NB = S // W
    sh12 = A.alloc([128, S], F32, "sh12")
    sh13 = A.alloc([128, S], F32, "sh13")
    t_sh = [Tok("sh%d" % i) for i in range(4)]
    mB0 = A.mark()
    wsh = [(A.alloc([128, 8, 128], BF16, "wsh"), Tok("wsh")) for _ in range(2)]
    prsb = [(A.alloc([128, 513], F32, "prs"), Tok("prs")) for _ in range(2)]
    t1sb = [(A.alloc([128, 512], F32, "t1s"), Tok("t1s")) for _ in range(2)]
    for ci, cc in enumerate((12, 13)):
        wb, twb = wsh[ci]
        P.dma(POOL, (lambda wb, cc: lambda e: e.dma_start(out=wb, in_=w_in_v[:, :, cc * 128:cc * 128 + 128]))(wb, cc),
              "wsh%d" % ci, writes=[twb])
    for ci, cc in enumerate((12, 13)):
        wb, twb = wsh[ci]
        P.memset(DVE, prsb[0][0][:, 0:1], 0.0, [prsb[0][1]])
        for b4 in range(4):
            prs, t_prs = prsb[b4 % 2]
            prn, t_prn = prsb[(b4 + 1) % 2]
            t1s, t_t1s = t1sb[b4 % 2]
            ps, tp = psum()
            for dc in range(8):
                P.mm(ps, wb[:, dc, :], hT[:, dc, b4 * 512:b4 * 512 + 512], [twb] + hT_all[b4 * 4:b4 * 4 + 4], [tp],
                     start=(dc == 0), stop=(dc == 7))
            P.copy(ACT, prs[:, 1:513], ps, [tp], [t_prs])
            P.copy(ACT, prn[:, 0:1], prs[:, 512:513], [t_prs], [t_prn])
            P.ts(DVE, t1s, prs[:, 0:512], pcols[:, MU0 + cc:MU0 + cc + 1], ALU.mult, [t_prs, tC], [t_t1s])
            dst = sh12 if cc == 12 else sh13
            sl = slice(b4 * 512, b4 * 512 + 512)
            P.stt(t1s, prs[:, 1:513], omm[:, cc:cc + 1], t1s, ALU.mult, ALU.add, [t_prs, t_t1s, tC], [t_t1s])
            if cc == 12:
                P.act(dst[0:64, sl], t1s[0:64, :], AF.Tanh, [t_t1s], [t_sh[b4]])
                P.copy(DVE, dst[64:128, sl], t1s[64:128, :], [t_t1s], [t_sh[b4]])
            else:
                P.act(dst[:, sl], t1s, AF.Sigmoid, [t_t1s], [t_sh[b4]])
    P.barrier()
    A.reset(mB0)

    if upto == 'B0':
        P.replay()
        return nc
    wrkv = [[(A.alloc([128, 8, 128], BF16, "wrkv"), Tok("wrkv")) for _ in range(3)] for _ in range(1)]
    MT_all = A.alloc([128, 16, 64], F32, "MT")
    Nn_all = A.alloc([128, 16, 64], F32, "Nn")
    RhT_all = A.alloc([128, 16, 128], F32, "RhT")
    YlT_all = A.alloc([128, 16, 128], BF16, "YlT")
    gw = A.alloc([128, S], BF16, "gw")
    bg = A.alloc([128, S], BF16, "bg")
    Hall = A.alloc([128, 17, 64], F32, "Hall")
    t_pp = [Tok("pp%d" % i) for i in range(NB)]
    t_H = Tok("H")
    pr = [[(A.alloc([128, 1 + W], F32, "pr"), Tok("pr")) for _ in range(2)] for _ in range(3)]

    def T_(shape, dt=F32, name="tmp"):
        return A.alloc(shape, dt, name), Tok(name)

    X1, t_X1 = T_([128, W], name="X1")
    X2, t_X2 = T_([128, W], name="X2")
    t1, t_t1 = X1, t_X1
    csx, t_csx = X1, t_X1
    bv, t_bv = X1, t_X1
    kk2, t_kk2 = X2, t_X2
    km, t_km = X2, t_X2
    rk, t_rk = X2, t_X2
    r_s, t_r = T_([128, W], name="r_s")
    k_s, t_k = T_([128, W], name="k_s")
    v_s, t_v = T_([128, W], name="v_s")
    sig, t_sig = T_([128, W], name="sig")
    a_s, t_a = T_([128, W], name="a_s")
    g_s, t_g = T_([128, W], name="g_s")
    cs, t_cs = T_([128, W], name="cs")
    Eex, t_Eex = T_([128, W], name="Eex")
    Eneg, t_Eneg = T_([128, W], name="Eneg")
    kk, t_kk = T_([128, W], name="kk")
    rn, t_rn = X2, t_X2
    BKh, t_BKh = T_([128, 2 * 256], BF16, name="BKh")
    ABT, t_ABT = T_([128, 4, 2, 128], BF16, name="ABT")
    AKT, t_AKT = T_([128, 4, 2, 128], BF16, name="AKT")
    u4 = lambda ap: ap.rearrange("p (u t) -> p u t", u=4)
    Lf = [T_([128, 512], BF16, name="L") for _ in range(2)]
    Pf = [T_([128, 512], BF16, name="Pm") for _ in range(2)]
    Lb = [(u4(a), t) for a, t in Lf]
    Pb = [(u4(a), t) for a, t in Pf]
    Zf_, t_Zf = T_([128, 512], F32, name="Zf")
    Zh_, t_Zh = T_([128, 512], BF16, name="Zh")
    Zf, Zh = u4(Zf_), u4(Zh_)
    v_bf, t_vbf = T_([128, W], BF16, name="v_bf")
    Ot = [T_([128, 512], F32, name="Ot") for _ in range(4)]
    v2 = lambda ap: ap.rearrange("p (c t) -> p c t", c=2)
    XS = []
    for i in range(2):
        d = {}
        d["AR"], d["t_AR"] = T_([128, 2 * 256], BF16, name="AR")
        d["BK"], d["t_BK"] = T_([128, 2 * 256], BF16, name="BK")
        d["TM"], d["t_TM"] = T_([128, 2, 4, 128], BF16, name="TM")
        d["Ein"], d["t_Ein"] = T_([128, W], name="Ein")
        d["gwb"], d["t_gwb"] = T_([128, W], BF16, name="gwb")
        d["bgb"], d["t_bgb"] = T_([128, W], BF16, name="bgb")
        d["AR4"] = d["AR"].rearrange("p (c k t) -> p c k t", c=2, k=2)
        d["BK4"] = d["BK"].rearrange("p (c k t) -> p c k t", c=2, k=2)
        XS.append(d)

    Yb, t_Yb = Ot[0]
    Yc, t_Yc = Ot[1]
    Ysq, t_Ysq = Ot[2]
    rstd, t_rstd = Ot[3]

    ring1 = [0]
    ring2 = [0]

    def psum1():
        bk = banks[ring1[0] % 4]
        ring1[0] += 1
        return bk

    zheld = [None]

    def psum2():
        bk = banks[4 + ring2[0] % 4]
        zheld[0] = 4 + ring2[0] % 4
        ring2[0] += 1
        return bk

    def psum2b():
        while 4 + ring2[0] % 4 == zheld[0]:
            ring2[0] += 1
        bk = banks[4 + ring2[0] % 4]
        ring2[0] += 1
        return bk

    def S1(p, b):
        X = XS[b % 2]
        AR, t_AR, BK, t_BK, TM, t_TM, Ein, t_Ein = X["AR"], X["t_AR"], X["BK"], X["t_BK"], X["TM"], X["t_TM"], X["Ein"], X["t_Ein"]
        AR4, BK4 = X["AR4"], X["BK4"]
        wset = wrkv[0]
        if b == 0:
            for role in range(3):
                wb, twb = wset[role]
                cc = role * 4 + p
                P.dma(POOL, (lambda wb, cc: lambda e: e.dma_start(out=wb, in_=w_in_v[:, :, cc * 128:cc * 128 + 128]))(wb, cc),
                      "wrkv_%d" % (role,), writes=[twb])
            for role in range(3):
                P.memset(DVE, pr[role][0][0][:, 0:1], 0.0, [pr[role][0][1]])
        tsl = slice(b * W, b * W + W)
        hdeps = hT_all[b * 2:b * 2 + 2]
        outs = [(r_s, t_r), (k_s, t_k), (v_s, t_v)]
        for role in range(3):
            wb, twb = wset[role]
            cc = role * 4 + p
            ps, tp = psum1()
            for dc in range(8):
                P.mm(ps[:, 0:W], wb[:, dc, :], hT[:, dc, tsl], [twb] + hdeps, [tp], start=(dc == 0), stop=(dc == 7))
            yield
            prb, tprb = pr[role][b % 2]
            prn, tprn = pr[role][(b + 1) % 2]
            P.copy(ACT, prb[:, 1:1 + W], ps[:, 0:W], [tp], [tprb])
            P.copy(ACT, prn[:, 0:1], prb[:, W:W + 1], [tprb], [tprn])
            P.ts(DVE, t1, prb[:, 0:W], pcols[:, MU0 + cc:MU0 + cc + 1], ALU.mult, [tprb, tC], [t_t1])
            o, to = outs[role]
            P.stt(o, prb[:, 1:1 + W], omm[:, cc:cc + 1], t1, ALU.mult, ALU.add, [tprb, t_t1, tC], [to])
            yield
        sdep = [t_sh[b // 2]]
        psw, tpw = psum1()
        P.mm(psw[:, 0:W], lora[0:64, p * 128:p * 128 + 128], sh12[0:64, tsl], [tC] + sdep, [tpw])
        psa, tpa = psum1()
        P.mm(psa[:, 0:W], lora[64:128, p * 128:p * 128 + 128], sh12[64:128, tsl], [tC] + sdep, [tpa])
        psg, tpg = psum1()
        P.mm(psg[:, 0:W], lora[:, 512 + p * 128:512 + p * 128 + 128], sh13[:, tsl], [tC] + sdep, [tpg])
        yield
        P.act(sig, psw[:, 0:W], AF.Exp, [tpw, tC], [t_sig], bias=nwa[:, p:p + 1], scale=-1.0)
        P.act(a_s, psa[:, 0:W], AF.Exp, [tpa, tC], [t_a], bias=nwa[:, 4 + p:4 + p + 1], scale=-1.0)
        P.copy(ACT, g_s, psg[:, 0:W], [tpg], [t_g])
        yield
        P.act(sig, sig, AF.Ln, [t_sig, tC], [t_sig], bias=onec)
        P.act(a_s, a_s, AF.Ln, [t_a, tC], [t_a], bias=onec)
        P.act(sig, sig, AF.Exp, [t_sig], [t_sig], scale=-1.0)
        P.act(a_s, a_s, AF.Exp, [t_a], [t_a], scale=-1.0)
        yield
        P.op(DVE, lambda e: e.tensor_tensor_scan(out=cs, data0=rmask, data1=sig, initial=0.0, op0=ALU.mult,
                                                 op1=ALU.add), [t_sig, tC], [t_cs])
        P.tt(DVE, csx, cs, sig, ALU.subtract, [t_cs, t_sig], [t_csx])
        yield
        P.act(Ein, cs, AF.Exp, [t_cs], [t_Ein], scale=-CEXP)
        P.act(Eex, csx, AF.Exp, [t_csx], [t_Eex], scale=-CEXP)
        P.act(Eneg, cs, AF.Exp, [t_cs], [t_Eneg], scale=CEXP)
        yield
        P.ts(DVE, kk, k_s, pcols[:, KK0 + p:KK0 + p + 1], ALU.mult, [t_k, tC], [t_kk])
        P.act(kk2, kk, AF.Square, [t_kk], [t_kk2])
        P.mm(psg[:, W:2 * W], BD1, kk2, [tC, t_kk2], [tpg])
        yield
        P.act(rn, psg[:, W:2 * W], AF.Ln, [tpg, tC], [t_rn], bias=tinyc)
        P.act(rn, rn, AF.Exp, [t_rn], [t_rn], scale=-0.5)
        P.tt(DVE, kk, kk, rn, ALU.mult, [t_kk, t_rn], [t_kk])
        yield
        P.ts(DVE, km, a_s, pcols[:, KA0 + p:KA0 + p + 1], ALU.mult, [t_a, tC], [t_km], s2=omka[:, p:p + 1], op1=ALU.add)
        P.tt(DVE, k_s, k_s, km, ALU.mult, [t_k, t_km], [t_k])
        P.tt(DVE, bv, kk, a_s, ALU.mult, [t_kk, t_a], [t_bv])
        yield
        P.stt(AR4[:, :, 0, :], v2(kk), -1.0, v2(Eex), ALU.mult, ALU.mult, [t_kk, t_Eex], [t_AR])
        P.tt(DVE, AR4[:, :, 1, :], v2(r_s), v2(Ein), ALU.mult, [t_r, t_Ein], [t_AR])
        yield
        P.tt(DVE, BK4[:, :, 0, :], v2(bv), v2(Eneg), ALU.mult, [t_bv, t_Eneg], [t_BK])
        P.tt(DVE, BK4[:, :, 1, :], v2(k_s), v2(Eneg), ALU.mult, [t_k, t_Eneg], [t_BK])
        yield
        for c in range(2):
            P.ts(DVE, BKh[:, c * 256:c * 256 + 256], BK[:, c * 256:c * 256 + 256], Ein[:, c * 128 + 127:c * 128 + 128],
                 ALU.mult, [t_BK, t_Ein], [t_BKh])
        yield
        P.stt(rk, r_s, pcols[:, RK0 + p:RK0 + p + 1], k_s, ALU.mult, ALU.mult, [t_r, t_k, tC], [t_rk])
        psb, tpb = psum1()
        P.mm(psb[:, 0:W], BD1, rk, [tC, t_rk], [tpb])
        yield
        P.tt(DVE, rk, psb[:, 0:W], v_s, ALU.mult, [tpb, t_v], [t_rk])
        P.ts(DVE, X["gwb"], g_s, pcols[:, LNW0 + p:LNW0 + p + 1], ALU.mult, [t_g, tC], [X["t_gwb"]])
        P.stt(X["bgb"], rk, pcols[:, LNB0 + p:LNB0 + p + 1], g_s, ALU.add, ALU.mult, [t_rk, t_g, tC], [X["t_bgb"]])
        yield
        P.copy(POOL, v_bf, v_s, [t_v], [t_vbf])
        for c in range(2):
            pst, tpt = psum1()
            srcs = [(AR[:, c * 256:c * 256 + 128], t_AR), (BKh[:, c * 256:c * 256 + 128], t_BKh),
                    (BKh[:, c * 256 + 128:c * 256 + 256], t_BKh), (v_bf[:, c * 128:c * 128 + 128], t_vbf)]
            pstb = pst.bitcast(BF16)
            for j, (sap, stok) in enumerate(srcs):
                P.tr(pstb[:, 128 * j:128 * j + 128], sap, identb, [stok, tC], [tpt])
            P.copy(ACT, TM[:, c, :, :], pstb[:, 0:512].rearrange("p (j t) -> p j t", j=4), [tpt], [t_TM])
            yield

    def S2(p, b):
        X = XS[b % 2]
        AR, t_AR, BK, t_BK, TM, t_TM, Ein, t_Ein = X["AR"], X["t_AR"], X["BK"], X["t_BK"], X["TM"], X["t_TM"], X["Ein"], X["t_Ein"]
        AR4 = X["AR4"]
        tsl = slice(b * W, b * W + W)
        P.copy(POOL, gw[:, tsl], X["gwb"], [X["t_gwb"]], [t_pp[b]])
        P.copy(POOL, bg[:, tsl], X["bgb"], [X["t_bgb"]], [t_pp[b]])
        units = [(c, q) for c in range(2) for q in range(2)]
        for which, dstT, tdst in ((0, ABT, t_ABT), (1, AKT, t_AKT)):
            pss = [psum2(), psum2()]
            for c in range(2):
                for q in range(2):
                    pq = slice(64 * q, 64 * q + 64)
                    ps, tp = pss[q]
                    lhs = BK[pq, c * 256 + which * 128:c * 256 + which * 128 + 128]
                    P.mm(ps[:, c * 256:c * 256 + 256], lhs, AR[pq, c * 256:c * 256 + 256], [t_BK, t_AR], [tp])
            for q in range(2):
                ps, tp = pss[q]
                P.tt(DVE, dstT[:, q::2, :, :], ps.rearrange("p (u k t) -> p u k t", u=2, k=2),
                     maskA[:, 0:2, :, :], ALU.mult, [tp, tC], [tdst])
            yield
        L0, tL0 = Lb[0]
        for q in range(2):
            pq = slice(64 * q, 64 * q + 64)
            ps, tp = psum2()
            for c in range(2):
                P.mm(ps[:, c * 128:c * 128 + 128], AR[pq, c * 256:c * 256 + 128], BK[pq, c * 256:c * 256 + 128],
                     [t_AR, t_BK], [tp])
            P.tt(DVE, L0[:, q::2, :], ps[:, 0:256].rearrange("p (u t) -> p u t", u=2), maskL[:, 0:2, :], ALU.mult,
                 [tp, tC], [tL0])
            yield
        psZ, tpZ = psum2()
        first = True
        for u, (c, q) in enumerate(units):
            P.mm(psZ[:, u * 128:u * 128 + 64], identb, TM[:, c, 0, 64 * q:64 * q + 64], [tC, t_TM], [tpZ],
                 start=first, stop=False, sgc=True)
            first = False
            P.mm(psZ[:, u * 128 + 64:u * 128 + 128], AKT[:, u, 0, :], TM[:, c, 3, 64 * q:64 * q + 64], [t_AKT, t_TM], [tpZ],
                 start=False, stop=(u == 3), sgc=True)
        yield
        P.copy(ACT, Zh_, psZ, [tpZ], [t_Zh])
        yield
        Pk, tPk = ABT, t_ABT
        for lev in range(7):
            Lk, tLk = Lb[lev % 2]
            Ln, tLn = Lb[(lev + 1) % 2]
            Pn, tPn = Pb[lev % 2]
            pk = (lambda u: ABT[:, u, 0, :]) if lev == 0 else (lambda u, Pk=Pk: Pk[:, u, :])
            if lev < 6:
                ps1, tp1 = psum2b()
                ps2, tp2 = psum2b()
                for u in range(4):
                    P.mm(ps1[:, u * 128:u * 128 + 128], Lk[:, u, :], pk(u), [tLk, tPk], [tp1])
                    P.mm(ps2[:, u * 128:u * 128 + 128], pk(u), Lk[:, u, :], [tLk, tPk], [tp2])
            for u in range(4):
                P.mm(psZ[:, u * 128:u * 128 + 128], pk(u), Zh[:, u, :], [tPk, t_Zh], [tpZ], start=False, stop=(u == 3), sgc=True)
            yield
            if lev < 6:
                P.copy(DVE, Pn, ps1.rearrange("p (u t) -> p u t", u=4), [tp1], [tPn])
                P.copy(ACT, Ln, ps2.rearrange("p (u t) -> p u t", u=4), [tp2], [tLn])
                Pk, tPk = Pn, tPn
            P.copy(ACT, Zh_, psZ, [tpZ], [t_Zh])
            yield
        Z7, tZ7 = Zh, t_Zh
        psM, tpM = psum2b()
        psR, tpR = psum2b()
        for u, (c, q) in enumerate(units):
            pq = slice(64 * q, 64 * q + 64)
            tpos = (0, 64 * q)
            P.mm(psM[pq, c * 64:c * 64 + 64], Z7[:, u, 0:64], TM[:, c, 1, 64 * q:64 * q + 64], [tZ7, t_TM], [tpM], tp=tpos)
            P.mm(psM[pq, 128 + c * 64:128 + c * 64 + 64], TM[:, c, 1, 64 * q:64 * q + 64], Z7[:, u, 64:128],
                 [tZ7, t_TM], [tpM], start=True, stop=False, tp=tpos)
            P.mm(psM[pq, 128 + c * 64:128 + c * 64 + 64], TM[:, c, 2, 64 * q:64 * q + 64], TM[:, c, 3, 64 * q:64 * q + 64],
                 [t_TM], [tpM], start=False, stop=True, tp=tpos)
            P.mm(psR[pq, c * 128:c * 128 + 128], Z7[:, u, 0:64], ABT[:, u, 1, :], [tZ7, t_ABT], [tpR], tp=tpos)
            P.mm(psR[pq, 256 + c * 128:256 + c * 128 + 128], Z7[:, u, 64:128], ABT[:, u, 1, :], [tZ7, t_ABT], [tpR],
                 start=True, stop=False, tp=tpos)
            P.mm(psR[pq, 256 + c * 128:256 + c * 128 + 128], TM[:, c, 3, 64 * q:64 * q + 64], AKT[:, u, 1, :],
                 [t_TM, t_AKT], [tpR], start=False, stop=True, tp=tpos)
        yield
        for c in range(2):
            P.stt(MT_all[:, 2 * b + c, :], identpair, Ein[:, c * 128 + 127:c * 128 + 128], psM[:, c * 64:c * 64 + 64],
                  ALU.mult, ALU.add, [tpM, t_Ein, tC], [t_pp[b]])
        P.copy(ACT, Nn_all[:, 2 * b:2 * b + 2, :], psM[:, 128:256].rearrange("p (c v) -> p c v", c=2), [tpM], [t_pp[b]])
        P.tt(DVE, RhT_all[:, 2 * b:2 * b + 2, :], psR[:, 0:256].rearrange("p (c t) -> p c t", c=2), AR4[:, :, 1, :],
             ALU.add, [tpR, t_AR], [t_pp[b]])
        P.copy(ACT, YlT_all[:, 2 * b:2 * b + 2, :], psR[:, 256:512].rearrange("p (c t) -> p c t", c=2), [tpR], [t_pp[b]])
        yield
        if b == NB - 1:
            yield from S3(p)

    def S3(p):
        P.memset(DVE, Hall[:, 0, :], 0.0, [t_H])
        for c in range(16):
            ps, tp = psum2()
            for q in range(2):
                pq = slice(64 * q, 64 * q + 64)
                P.mm(ps[pq, 0:64], MT_all[pq, c, :], Hall[pq, c, :], [t_pp[c // 2], t_H], [tp], tp=(64 * q, 64 * q))
            P.tt(DVE, Hall[:, c + 1, :], ps[:, 0:64], Nn_all[:, c, :], ALU.add, [tp, t_pp[c // 2]], [t_H])
            yield
        for b4 in range(4):
            ps, tp = psum2()
            for cj in range(4):
                c = b4 * 4 + cj
                for q in range(2):
                    pq = slice(64 * q, 64 * q + 64)
                    P.mm(ps[pq, cj * 128:cj * 128 + 128], Hall[pq, c, :], RhT_all[pq, c, :], [t_H, t_pp[c // 2]], [tp],
                         tp=(64 * q, 64 * q))
            P.tt(DVE, u4(Yb), u4(ps), YlT_all[:, b4 * 4:b4 * 4 + 4, :], ALU.add,
                 [tp, t_pp[b4 * 2], t_pp[b4 * 2 + 1]], [t_Yb])
            if p == 0 and b4 == 0:
                dump("Y0", Yb, t_Yb)
            yield
            psm, tpm = psum2()
            P.mm(psm, BD64, Yb, [tC, t_Yb], [tpm])
            P.tt(DVE, Yc, Yb, psm, ALU.subtract, [t_Yb, tpm], [t_Yc])
            P.act(Ysq, Yc, AF.Square, [t_Yc], [t_Ysq])
            yield
            psv, tpv = psum2()
            P.mm(psv, BD64, Ysq, [tC, t_Ysq], [tpv])
            P.act(rstd, psv, AF.Ln, [tpv, tC], [t_rstd], bias=epsG)
            P.act(rstd, rstd, AF.Exp, [t_rstd], [t_rstd], scale=-0.5)
            yield
            sl = slice(b4 * 512, b4 * 512 + 512)
            P.tt(DVE, Yc, Yc, rstd, ALU.mult, [t_Yc, t_rstd], [t_Yc])
            P.tt(DVE, Yc, Yc, gw[:, sl], ALU.mult, [t_Yc, t_pp[b4 * 2], t_pp[b4 * 2 + 1]], [t_Yc])
            P.tt(DVE, yT[:, p, sl], Yc, bg[:, sl], ALU.add, [t_Yc, t_pp[b4 * 2], t_pp[b4 * 2 + 1]], [t_yT[p]])
            yield

    NBLK = 4 * NB
    if upto.startswith('B') and len(upto) == 3:
        NBLK = int(upto[2])
    i1 = 0
    f1 = 0
    f2 = 0
    g1 = None
    g2 = None
    while f2 < NBLK:
        if g1 is None and i1 < NBLK and i1 - f2 <= 1:
            g1 = S1(i1 // NB, i1 % NB)
            i1 += 1
        if g2 is None and f2 < f1:
            g2 = S2(f2 // NB, f2 % NB)
        if g1 is not None:
            try:
                next(g1)
            except StopIteration:
                g1 = None
                f1 += 1
        if g2 is not None:
            try:
                next(g2)
            except StopIteration:
                g2 = None
                f2 += 1
    dump("yT0", yT[:, 0, :], t_yT[0])
    P.barrier()
    A.reset(phase_mark)
    if upto == 'B':
        P.replay()
        return nc

    A.cur += 16 * D * 4
    wo, t_wo = T_([128, 8, D], BF16, "wo")
    wo_end = A.mark()
    A.reset(phase_mark + 16 * D * 4)
    wd, t_wdd = T_([128, NFF, D], BF16, "wd")
    t_wd = [Tok("wd%d" % j) for j in range(NFF)]
    NPRE = 12
    A.reset(phase_mark)
    for half in range(2):
        P.dma(POOL, (lambda half: lambda e: e.dma_start(out=wo[:, :, half * 512:half * 512 + 512],
                                                        in_=w_out_v[:, :, half * 512:half * 512 + 512]))(half),
              "wo", writes=[t_wo])
    watt = [[(A.alloc([128, 8, 128], BF16, "watt"), Tok("watt")) for _ in range(3)] for _ in range(2)]
    QT, t_QT = T_([128, S], BF16, "QT")
    KT, t_KT = T_([128, S], BF16, "KT")
    Vpad = [T_([128, 16, 2, 128], BF16, "Vpad") for _ in range(2)]
    acc_o, t_acco = T_([128, S], F32, "acc_o")
    acc_d, t_accd = T_([128, S], F32, "acc_d")
    peT = [T_([128, 512], BF16, "peT") for _ in range(4)]
    otmp, t_otmp = T_([128, 512], F32, "otmp")
    osq, t_osq = T_([128, 512], F32, "osq")
    orst, t_orst = osq, t_osq
    VT, t_VT = T_([128, S], BF16, "VT")
    for vb, tvb in Vpad:
        P.memset(POOL, vb, 0.0, [tvb])

    def tokset(br, blk):
        if br == 0:
            return 128 * blk, 1
        if br == 1:
            r2, n2 = blk // 4, blk % 4
            return 512 * n2 + r2, 4
        return blk, 16

    def tsl_(br, blk):
        st, sp = tokset(br, blk)
        return slice(st, st + sp * 127 + 1, sp)

    def has_prev(br, blk):
        if br == 0:
            return blk >= 1
        if br == 1:
            return blk % 4 >= 1
        return False

    ringC = [0]

    def psumC():
        bk = banks[4 + ringC[0] % 4]
        ringC[0] += 1
        return bk

    vcount = 0
    pecount = 0
    gcount = 0
    for p in range(4):
        wset = watt[p % 2]
        for role in range(3):
            wb, twb = wset[role]
            cc = 14 + role * 4 + p
            P.dma(POOL, (lambda wb, cc: lambda e: e.dma_start(out=wb, in_=w_in_v[:, :, cc * 128:cc * 128 + 128]))(wb, cc),
                  "watt%d_%d" % (p % 2, role), writes=[twb])
        if p == 1:
            for j in range(NPRE, NFF):
                P.dma(POOL, (lambda j: lambda e: e.dma_start(out=wd[:, j, :], in_=w_down_v[:, j, :]))(j), "wd%d" % j,
                      writes=[t_wd[j]])
        for role, (dst, tdst) in enumerate(((QT, t_QT), (KT, t_KT))):
            wb, twb = wset[role]
            for b4 in range(4):
                ps, tp = psumC()
                for dc in range(8):
                    P.mm(ps, wb[:, dc, :], hT[:, dc, b4 * 512:b4 * 512 + 512], [twb] + hT_all[b4 * 4:b4 * 4 + 4], [tp],
                         start=(dc == 0), stop=(dc == 7))
                P.act(dst[:, b4 * 512:b4 * 512 + 512], ps, AF.Copy, [tp], [tdst], scale=(0.125 if role == 0 else 1.0))
        wv, twv = wset[2]
        vbufs = {}

        for b4 in range(4):
            ps, tp = psumC()
            for dc in range(8):
                P.mm(ps, wv[:, dc, :], hT[:, dc, b4 * 512:b4 * 512 + 512], [twv] + hT_all[b4 * 4:b4 * 4 + 4], [tp],
                     start=(dc == 0), stop=(dc == 7))
            P.copy(DVE, VT[:, b4 * 512:b4 * 512 + 512], ps, [tp], [t_VT])

        def vproj(br):
            nonlocal vcount
            vb, tvb = Vpad[vcount % 2]
            vcount += 1
            vbufs[br] = (vb, tvb)
            for g4 in range(4):
                ps, tp = psumC()
                psb = ps.bitcast(BF16)
                for j in range(4):
                    blk = g4 * 4 + j
                    P.tr(psb[:, j * 128:j * 128 + 128], VT[:, tsl_(br, blk)], identb, [t_VT, tC], [tp])
                psv4 = psb[:, 0:512].rearrange("p (j q e) -> p j q e", j=4, q=2)
                for q in range(2):
                    P.copy(ACT if q == 0 else DVE, vb[:, g4 * 4:g4 * 4 + 4, q, 64 * q:64 * q + 64], psv4[:, :, q, :], [tp], [tvb])

        def Xs(br, g4, jp):
            nonlocal pecount
            js = (2 * jp, 2 * jp + 1)
            blks = [g4 * 4 + j for j in js]
            hps = [has_prev(br, blk) for blk in blks]
            mb = MB_TT if (hps[0] and hps[1]) else (MB_FT if hps[1] else MB_FF)
            pes = []
            pss = [psumC(), psumC()]
            for ji in range(2):
                blk = blks[ji]
                qs = tsl_(br, blk)
                if hps[ji]:
                    for q in range(2):
                        pq = slice(64 * q, 64 * q + 64)
                        psS, tpS = pss[q]
                        P.mm(psS[:, ji * 256:ji * 256 + 128], KT[pq, tsl_(br, blk - 1)], QT[pq, qs], [t_KT, t_QT], [tpS])
                for q in range(2):
                    pq = slice(64 * q, 64 * q + 64)
                    psS, tpS = pss[q]
                    P.mm(psS[:, ji * 256 + 128:ji * 256 + 256], KT[pq, qs], QT[pq, qs], [t_KT, t_QT], [tpS])
            for q in range(2):
                psS, tpS = pss[q]
                pe, tpe = peT[pecount % 4]
                pecount += 1
                if hps[0] and hps[1]:
                    v = lambda a: a
                elif hps[1]:
                    v = lambda a: a[:, 128:512]
                else:
                    v = lambda a: a.rearrange("p (j k t) -> p j k t", j=2, k=2)[:, :, 1, :]
                P.act(v(pe), v(psS), AF.Exp, [tpS], [tpe])
                P.tt(DVE, v(pe), v(pe), v(mb), ALU.mult, [tpe, tC], [tpe])
                pes.append((pe, tpe))
            return (js, blks, hps, pes)

        gstate = {}

        def Ys(br, g4, jp, xs):
            nonlocal gcount
            js, blks, hps, pes = xs
            vb, tvb = vbufs[br]
            if jp == 0:
                pb = (gcount % 2) * 2
                gcount += 1
                gstate[(br, g4)] = (banks[pb], banks[pb + 1])
            (psO, tpO), (psD, tpD) = gstate[(br, g4)]
            for ji in range(2):
                j = js[ji]
                blk = blks[ji]
                mms = []
                for q in range(2):
                    pe, tpe = pes[q]
                    if hps[ji]:
                        mms.append((q, blk - 1, pe[:, ji * 256:ji * 256 + 128], tpe))
                    mms.append((q, blk, pe[:, ji * 256 + 128:ji * 256 + 256], tpe))
                for i, (q, kb, rhs, tpe) in enumerate(mms):
                    P.mm(psO[:, j * 128:j * 128 + 128], vb[:, kb, q, :], rhs, [tvb, tpe], [tpO],
                         start=(i == 0), stop=(i == len(mms) - 1))
                for i, (q, kb, rhs, tpe) in enumerate(mms):
                    P.mm(psD[:, j * 128:j * 128 + 128], onespad[:, q, :], rhs, [tC, tpe], [tpD],
                         start=(i == 0), stop=(i == len(mms) - 1))
            if jp == 1:
                if br == 0:
                    oa = acc_o[:, g4 * 512:g4 * 512 + 512]
                    da = acc_d[:, g4 * 512:g4 * 512 + 512]
                    P.copy(ACT, oa, psO, [tpO], [t_acco])
                    P.copy(DVE, da, psD, [tpD], [t_accd])
                else:
                    if br == 1:
                        view = lambda a: a.rearrange("p (n i r) -> p r n i", n=4, r=4)[:, g4, :, :]
                    else:
                        view = lambda a: a.rearrange("p (i r) -> p r i", r=16)[:, g4 * 4:g4 * 4 + 4, :]
                    P.tt(DVE, view(acc_o), view(acc_o), psO.rearrange("p (j i) -> p j i", j=4), ALU.add, [tpO, t_acco], [t_acco])
                    P.tt(DVE, view(acc_d), view(acc_d), psD.rearrange("p (j i) -> p j i", j=4), ALU.add, [tpD, t_accd], [t_accd])

        U = [(br, g4, jp) for br in range(3) for g4 in range(4) for jp in range(2)]
        vproj(0)
        xs_next = Xs(*U[0])
        for k in range(len(U)):
            xs_cur = xs_next
            if k + 1 < len(U):
                if U[k + 1][0] != U[k][0]:
                    vproj(U[k + 1][0])
                xs_next = Xs(*U[k + 1])
            Ys(*U[k], xs_cur)
        for b4 in range(4):
            sl = slice(b4 * 512, b4 * 512 + 512)
            P.act(otmp, acc_d[:, sl], AF.Ln, [t_accd], [t_otmp])
            P.act(otmp, otmp, AF.Exp, [t_otmp], [t_otmp], scale=-1.0)
            P.tt(DVE, otmp, otmp, acc_o[:, sl], ALU.mult, [t_otmp, t_acco], [t_otmp])
            P.act(osq, otmp, AF.Square, [t_otmp], [t_osq])
            ps, tp = psumC()
            P.mm(ps, BD64, osq, [tC, t_osq], [tp])
            P.act(orst, ps, AF.Ln, [tp, tC], [t_orst], bias=epsN)
            P.act(orst, orst, AF.Exp, [t_orst], [t_orst], scale=-0.5)
            P.stt(yT[:, 4 + p, sl], otmp, pcols[:, OG0 + p:OG0 + p + 1], orst, ALU.mult, ALU.mult, [t_otmp, t_orst, tC],
                  [t_yT[4 + p]])
    dump("yT4", yT[:, 4, :], t_yT[4])
    assert A.mark() <= phase_mark + 16 * D * 4, "phase C buffers overlap the prefetched w_out"
    P.barrier()
    A.reset(phase_mark)
    if upto == 'C':
        P.replay()
        return nc

    x1, t_x1d = T_([128, 16, D], F32, "x1")
    t_x1 = [Tok("x1_%d" % i) for i in range(16)]
    x1_end = A.mark()
    A.reset(wo_end)
    xr = [T_([128, D], F32, "xr") for _ in range(2)]
    for tt_ in range(16):
        xa, txa = xr[tt_ % 2]
        P.dma(SP, (lambda xa, i: lambda e: e.dma_start(out=xa, in_=x_v[:, i, :]))(xa, tt_), "xr%d" % (tt_ % 2), writes=[txa])
        for ch in range(2):
            ps, tp = psum()
            for cc in range(8):
                P.mm(ps, yT[:, cc, tt_ * 128:tt_ * 128 + 128], wo[:, cc, ch * 512:ch * 512 + 512], [t_yT[cc], t_wo], [tp],
                     start=(cc == 0), stop=(cc == 7))
            P.tt(DVE, x1[:, tt_, ch * 512:ch * 512 + 512], ps, xa[:, ch * 512:ch * 512 + 512], ALU.add, [tp, txa], [t_x1[tt_]])
    dump("x1", x1[:, 0, :], t_x1[0])
    P.barrier()
    A.reset(persist_mark)
    h2Ts = [T_([128, 8, 512], BF16, "h2T") for _ in range(2)]
    aT, t_aTd = T_([128, NFF, 512], BF16, "aT")
    t_aT = [Tok("aT%d" % i) for i in range(NFF)]
    scrD = [(A.alloc([128, D], F32, "hn"), Tok("hn"), A.alloc([128, D], BF16, "sq"), Tok("sq")) for _ in range(1)]
    wgu = [[T_([128, 8, 128], BF16, "wgu") for _ in range(2)] for _ in range(3)]
    sg = [T_([128, 512], BF16, "sg") for _ in range(2)]
    xo = [T_([128, D], F32, "xo") for _ in range(1)]
    assert A.mark() <= phase_mark, "overlay exceeds dead hT/yT region"
    A.reset(x1_end)
    assert A.mark() == phase_mark + 16 * D * 4
    ncol = 16

    def norm_q(qt):
        nonlocal ncol
        h2T, t_h2 = h2Ts[qt % 2]
        for tl in range(4):
            tt_ = qt * 4 + tl
            norm_p1(x1[:, tt_, :], t_x1[tt_], g2bc, scrD[0], ncol)
            ncol += 1
            norm_p2(scrD[0], h2T, tl * 128, t_h2, psum)

    gucount = 0
    norm_q(0)
    for qt in range(4):
        h2T, t_h2 = h2Ts[qt % 2]
        for j in range(NFF):
            wg_, twg = wgu[gucount % 3][0]
            wu_, twu = wgu[gucount % 3][1]
            key = gucount % 3
            gucount += 1
            P.dma(POOL, (lambda wg_, j: lambda e: e.dma_start(out=wg_, in_=w_gate_v[:, :, j * 128:j * 128 + 128]))(wg_, j),
                  "wg%d" % key, writes=[twg])
            P.dma(POOL, (lambda wu_, j: lambda e: e.dma_start(out=wu_, in_=w_up_v[:, :, j * 128:j * 128 + 128]))(wu_, j),
                  "wu%d" % key, writes=[twu])
            if qt == 0 and j < NPRE:
                P.dma(POOL, (lambda j: lambda e: e.dma_start(out=wd[:, j, :], in_=w_down_v[:, j, :]))(j), "wd%d" % j,
                      writes=[t_wd[j]])
            psg, tpg = psum()
            psu, tpu = psum()
            for dc in range(8):
                P.mm(psg, wg_[:, dc, :], h2T[:, dc, :], [twg, t_h2], [tpg], start=(dc == 0), stop=(dc == 7))
            for dc in range(8):
                P.mm(psu, wu_[:, dc, :], h2T[:, dc, :], [twu, t_h2], [tpu], start=(dc == 0), stop=(dc == 7))
            sgb, tsg = sg[j % 2]
            P.act(sgb, psg, AF.Silu, [tpg], [tsg])
            P.tt(DVE, aT[:, j, :], sgb, psu, ALU.mult, [tsg, tpu], [t_aT[j]])
        if qt + 1 < 4:
            norm_q(qt + 1)
        for tl in range(4):
            tt_ = qt * 4 + tl
            xob, txo = xo[0]
            for ch in range(2):
                ps, tp = psum()
                for j in range(NFF):
                    P.mm(ps, aT[:, j, tl * 128:tl * 128 + 128], wd[:, j, ch * 512:ch * 512 + 512], [t_aT[j], t_wd[j]], [tp],
                         start=(j == 0), stop=(j == NFF - 1))
                P.tt(DVE, xob[:, ch * 512:ch * 512 + 512], ps, x1[:, tt_, ch * 512:ch * 512 + 512], ALU.add,
                     [tp, t_x1[tt_]], [txo])
            hn_, t_hn_, sc, tsc = scrD[0]
            tS_ = tSs[ncol]
            P.act(sc, xob, AF.Square, [txo], [tsc, tS_], accum=ssq[:, ncol:ncol + 1])
            P.act(ssq[:, ncol:ncol + 1], ssq[:, ncol:ncol + 1], AF.Ln, [tS_, tC], [tS_], bias=epsN, scale=1.0 / D)
            P.act(ssq[:, ncol:ncol + 1], ssq[:, ncol:ncol + 1], AF.Exp, [tS_], [tS_], scale=-0.5)
            P.stt(xob, xob, ssq[:, ncol:ncol + 1], gfbc, ALU.mult, ALU.mult, [txo, tS_, tC], [txo])
            ncol += 1
            P.dma(SP, (lambda xob, i: lambda e: e.dma_start(out=out_v[:, i, :], in_=xob))(xob, tt_), "xo0",
                  reads=[txo], is_out=True)
    P.replay()
    return nc


_NC = {}


def _host_layout(inp):
    f = lambda a: np.ascontiguousarray(np.asarray(a, dtype=np.float32))
    col = lambda v: f(v).reshape(-1, 128).T
    pcols = np.concatenate([
        col(inp["mu_shift"][0]), col(inp["decay_w0"][0]), col(inp["iclr_a0"][0]), col(inp["k_k"][0]),
        col(inp["k_a"][0]), col(inp["r_k"][0].reshape(-1)), col(inp["ln_x_w"][0]), col(inp["ln_x_b"][0]),
        col(inp["attn_out_g"][0])], axis=1)
    assert pcols.shape == (128, NPC)
    gbc = np.stack([np.broadcast_to(f(inp["mix_norm_g"][0])[None, :], (128, D)),
                    np.broadcast_to(f(inp["ffn_norm_g"][0])[None, :], (128, D)),
                    np.broadcast_to(f(inp["final_norm_g"])[None, :], (128, D))], axis=0)
    lora = np.concatenate([np.concatenate([f(inp["decay_w2"][0]), f(inp["iclr_a2"][0])], axis=0),
                           f(inp["gate_g2"][0])], axis=1)
    shared = {
        "w_in": f(inp["w_in"][0]), "w_out": f(inp["w_out"][0]), "w_gate": f(inp["w_gate"][0]),
        "w_up": f(inp["w_up"][0]), "w_down": f(inp["w_down"][0]),
        "pcols": f(pcols), "gbc": f(gbc), "lora": f(lora),
    }
    return shared


def kernel(**inputs):
    x = np.asarray(inputs["x"], dtype=np.float32)
    shared = _host_layout(inputs)
    if "nc" not in _NC:
        _NC["nc"] = build()
    nc = _NC["nc"]
    in_maps = []
    for b in range(8):
        m = dict(shared)
        m["x"] = np.ascontiguousarray(x[b])
        in_maps.append(m)
    res = run_bass_kernel_spmd(nc, in_maps, core_ids=list(range(8)))
    return np.stack([r["out"] for r in res.results], axis=0).astype(np.float32)
```
